# Optimizing a Trainium2 kernel written in Bass

```python
import math
import jax, jax.numpy as jnp
from jax import lax
import numpy as np

D_MODEL = 1024
BATCH = 4
SEQ = 8192
DEPTH = 1

PLE_DIM = 256
EPS = 1e-6
GDN_HEADS = 4
GDN_DK = 128
GDN_DV = 128
CONV_K = 4
CHUNK = 64
DIFF_HEADS = 4
DIFF_DH = 64
DIFF_DV = 2 * DIFF_DH
ROT_DIM = DIFF_DH // 4
ROPE_THETA = 500000.0
Q_BLOCK = 128
D_FF = 4 * D_MODEL

GDN_QK = GDN_HEADS * GDN_DK
GDN_V = GDN_HEADS * GDN_DV
DIFF_QK = DIFF_HEADS * 2 * DIFF_DH
DIFF_V = DIFF_HEADS * DIFF_DV
IN_SIZES = (GDN_QK, GDN_QK, GDN_V, GDN_V, GDN_HEADS, GDN_HEADS, DIFF_QK, DIFF_QK, DIFF_V, D_MODEL, D_MODEL)
D_IN = int(sum(IN_SIZES))
IN_SPLITS = tuple(int(s) for s in np.cumsum(IN_SIZES)[:-1])

kernel_name = 'hybrid_gdn_diffattn_gated_merge'


def rms_norm(x, gain):
    xf = x.astype(jnp.float32)
    y = xf * lax.rsqrt(jnp.mean(xf * xf, axis=-1, keepdims=True) + EPS)
    return (y * gain.astype(jnp.float32)).astype(x.dtype)


def l2_normalize(x):
    xf = x.astype(jnp.float32)
    return xf * lax.rsqrt(jnp.sum(xf * xf, axis=-1, keepdims=True) + EPS)


def causal_depthwise_conv(x, w):
    c = x.shape[-1]
    return lax.conv_general_dilated(
        x, w[:, None, :].astype(x.dtype), window_strides=(1,), padding=[(CONV_K - 1, 0)],
        dimension_numbers=('NWC', 'WIO', 'NWC'), feature_group_count=c)


def apply_partial_rotary(x, cos, sin):
    half = ROT_DIM // 2
    c = cos[:, :, None, None, :].astype(x.dtype)
    s = sin[:, :, None, None, :].astype(x.dtype)
    x1, x2, xp = x[..., :half], x[..., half:ROT_DIM], x[..., ROT_DIM:]
    return jnp.concatenate([x1 * c - x2 * s, x2 * c + x1 * s, xp], axis=-1)


def chunked_gated_delta_rule(q, k, v, g, beta):
    b, t, h, dk = q.shape
    dv = v.shape[-1]
    n = t // CHUNK

    def to_chunks(a):
        a = a.astype(jnp.float32).reshape((b, n, CHUNK, h) + a.shape[3:])
        return jnp.moveaxis(a, 3, 1)

    q = to_chunks(q) * (dk ** -0.5)
    k = to_chunks(k)
    v = to_chunks(v)
    beta = to_chunks(beta)
    g = jnp.cumsum(to_chunks(g), axis=-1)
    idx = jnp.arange(CHUNK)
    incl = idx[:, None] >= idx[None, :]
    strict = idx[:, None] > idx[None, :]
    gdiff = g[..., :, None] - g[..., None, :]
    decay = jnp.where(incl, jnp.exp(jnp.where(incl, gdiff, 0.0)), 0.0)
    kb = k * beta[..., None]
    lower = jnp.where(strict, jnp.einsum('bhncd,bhnsd->bhncs', kb, k) * decay, 0.0)
    eye = jnp.eye(CHUNK, dtype=jnp.float32)
    tmat = lax.linalg.triangular_solve(eye + lower, jnp.broadcast_to(eye, lower.shape),
                                       left_side=True, lower=True, unit_diagonal=True)
    u = jnp.einsum('bhncs,bhnsd->bhncd', tmat, v * beta[..., None])
    w = jnp.einsum('bhncs,bhnsd->bhncd', tmat, kb * jnp.exp(g)[..., None])
    a_intra = jnp.einsum('bhncd,bhnsd->bhncs', q, k) * decay
    g_last = g[..., -1]
    q_dec = q * jnp.exp(g)[..., None]
    k_dec = k * jnp.exp(g_last[..., None] - g)[..., None]

    def step(state, xs):
        q_i, k_i, u_i, w_i, a_i, gl_i = xs
        v_new = u_i - jnp.einsum('bhck,bhkv->bhcv', w_i, state)
        o_i = jnp.einsum('bhck,bhkv->bhcv', q_i, state) + jnp.einsum('bhcs,bhsv->bhcv', a_i, v_new)
        state = state * jnp.exp(gl_i)[..., None, None] + jnp.einsum('bhck,bhcv->bhkv', k_i, v_new)
        return state, o_i

    xs = tuple(jnp.moveaxis(a, 2, 0) for a in (q_dec, k_dec, u, w, a_intra, g_last))
    s0 = jnp.zeros((b, h, dk, dv), jnp.float32)
    _, o = lax.scan(step, s0, xs)
    o = jnp.moveaxis(o, 0, 2)
    return jnp.moveaxis(o, 1, 3).reshape(b, t, h, dv)


def diff_attention(q, k, v, lam):
    b, t, h, _, dh = q.shape
    nb = t // Q_BLOCK
    qf = q.astype(jnp.float32) * (dh ** -0.5)
    qb = jnp.moveaxis(qf.reshape(b, nb, Q_BLOCK, h, 2, dh), 1, 0)
    kf = k.astype(jnp.float32)
    vf = v.astype(jnp.float32)
    key_idx = jnp.arange(t)

    def block(args):
        q_blk, blk = args
        q_idx = blk * Q_BLOCK + jnp.arange(Q_BLOCK)
        s = jnp.einsum('bqhcd,bkhcd->bhcqk', q_blk, kf)
        mask = key_idx[None, :] <= q_idx[:, None]
        pr = jax.nn.softmax(jnp.where(mask, s, -jnp.inf), axis=-1)
        a = pr[:, :, 0] - lam * pr[:, :, 1]
        return jnp.einsum('bhqk,bkhv->bqhv', a, vf)

    o = lax.map(block, (qb, jnp.arange(nb)))
    return jnp.moveaxis(o, 0, 1).reshape(b, t, h, v.shape[-1])


def setup_inputs(seed: int = 0) -> dict:
    key = jax.random.key(seed)
    ks = jax.random.split(key, 26)

    def nrm(k, shape, scale):
        return jax.random.normal(k, shape, jnp.float32) * scale

    def gain(k, shape):
        return 1.0 + 0.1 * jax.random.normal(k, shape, jnp.float32)

    offsets = jax.random.randint(ks[2], (BATCH, 1), 0, 1024, dtype=jnp.int32)
    positions = offsets + jnp.arange(SEQ, dtype=jnp.int32)[None, :]
    dt = jnp.exp(jax.random.uniform(ks[7], (DEPTH, GDN_HEADS), jnp.float32)
                 * (math.log(0.1) - math.log(0.001)) + math.log(0.001))
    dt_bias = dt + jnp.log(-jnp.expm1(-dt))
    a_log = jnp.log(jax.random.uniform(ks[6], (DEPTH, GDN_HEADS), jnp.float32, 1.0, 16.0))
    return {
        'x': nrm(ks[0], (BATCH, SEQ, D_MODEL), 1.0),
        'p': nrm(ks[1], (DEPTH, BATCH, SEQ, PLE_DIM), 1.0),
        'positions': positions,
        'attn_norm': gain(ks[3], (DEPTH, D_MODEL)),
        'w_in': nrm(ks[4], (DEPTH, D_MODEL, D_IN), D_MODEL ** -0.5),
        'conv_w': nrm(ks[5], (DEPTH, CONV_K, GDN_QK * 2 + GDN_V), CONV_K ** -0.5),
        'a_log': a_log,
        'dt_bias': dt_bias,
        'gdn_norm': gain(ks[8], (DEPTH, GDN_DV)),
        'w_o_a': nrm(ks[9], (DEPTH, GDN_V, D_MODEL), GDN_V ** -0.5),
        'q_norm': gain(ks[10], (DEPTH, DIFF_DH)),
        'k_norm': gain(ks[11], (DEPTH, DIFF_DH)),
        'lambda_q1': nrm(ks[12], (DEPTH, DIFF_DH), 0.1),
        'lambda_k1': nrm(ks[13], (DEPTH, DIFF_DH), 0.1),
        'lambda_q2': nrm(ks[14], (DEPTH, DIFF_DH), 0.1),
        'lambda_k2': nrm(ks[15], (DEPTH, DIFF_DH), 0.1),
        'diff_norm': gain(ks[16], (DEPTH, DIFF_DV)),
        'w_o_b': nrm(ks[17], (DEPTH, DIFF_V, D_MODEL), DIFF_V ** -0.5),
        'w_out': nrm(ks[18], (DEPTH, D_MODEL, D_MODEL), D_MODEL ** -0.5),
        'mlp_norm': gain(ks[19], (DEPTH, D_MODEL)),
        'w_up': nrm(ks[20], (DEPTH, D_MODEL, D_FF), D_MODEL ** -0.5),
        'w_down': nrm(ks[21], (DEPTH, D_FF, D_MODEL), D_FF ** -0.5),
        'ple_norm': gain(ks[22], (DEPTH, D_MODEL)),
        'w_ple_gate': nrm(ks[23], (DEPTH, D_MODEL, D_MODEL), D_MODEL ** -0.5),
        'w_ple': nrm(ks[24], (DEPTH, PLE_DIM, D_MODEL), PLE_DIM ** -0.5),
    }


def reference(x, p, positions, attn_norm, w_in, conv_w, a_log, dt_bias, gdn_norm, w_o_a,
              q_norm, k_norm, lambda_q1, lambda_k1, lambda_q2, lambda_k2, diff_norm, w_o_b,
              w_out, mlp_norm, w_up, w_down, ple_norm, w_ple_gate, w_ple):
    b, t, _ = x.shape
    inv_freq = ROPE_THETA ** (-jnp.arange(0, ROT_DIM, 2, dtype=jnp.float32) / ROT_DIM)
    ang = positions.astype(jnp.float32)[..., None] * inv_freq
    cos, sin = jnp.cos(ang), jnp.sin(ang)

    for i in range(DEPTH):
        h = rms_norm(x, attn_norm[i])
        proj = h @ w_in[i].astype(x.dtype)
        (a_q, a_k, a_v, a_z, a_a, a_b, b_q, b_k, b_v, gate_a, gate_b) = jnp.split(proj, IN_SPLITS, axis=-1)

        qkv = jax.nn.silu(causal_depthwise_conv(jnp.concatenate([a_q, a_k, a_v], axis=-1), conv_w[i]))
        cq, ck, cv = jnp.split(qkv, [GDN_QK, 2 * GDN_QK], axis=-1)
        cq = l2_normalize(cq.reshape(b, t, GDN_HEADS, GDN_DK))
        ck = l2_normalize(ck.reshape(b, t, GDN_HEADS, GDN_DK))
        cv = cv.reshape(b, t, GDN_HEADS, GDN_DV)
        g = -jnp.exp(a_log[i].astype(jnp.float32)) * jax.nn.softplus(
            a_a.astype(jnp.float32) + dt_bias[i].astype(jnp.float32))
        beta = jax.nn.sigmoid(a_b.astype(jnp.float32))
        o_a = chunked_gated_delta_rule(cq, ck, cv, g, beta)
        o_a = rms_norm(o_a, gdn_norm[i]) * jax.nn.silu(
            a_z.reshape(b, t, GDN_HEADS, GDN_DV).astype(jnp.float32))
        y_a = o_a.reshape(b, t, GDN_V).astype(x.dtype) @ w_o_a[i].astype(x.dtype)

        bq = rms_norm(b_q.reshape(b, t, DIFF_HEADS, 2, DIFF_DH), q_norm[i])
        bk = rms_norm(b_k.reshape(b, t, DIFF_HEADS, 2, DIFF_DH), k_norm[i])
        bq = apply_partial_rotary(bq, cos, sin)
        bk = apply_partial_rotary(bk, cos, sin)
        lam_init = 0.8 - 0.6 * math.exp(-0.3 * i)
        lam = (jnp.exp(jnp.sum(lambda_q1[i].astype(jnp.float32) * lambda_k1[i].astype(jnp.float32)))
               - jnp.exp(jnp.sum(lambda_q2[i].astype(jnp.float32) * lambda_k2[i].astype(jnp.float32)))
               + lam_init)
        o_b = diff_attention(bq, bk, b_v.reshape(b, t, DIFF_HEADS, DIFF_DV), lam)
        o_b = rms_norm(o_b, diff_norm[i]) * (1.0 - lam_init)
        y_b = o_b.reshape(b, t, DIFF_V).astype(x.dtype) @ w_o_b[i].astype(x.dtype)

        merged = jax.nn.sigmoid(gate_a) * y_a + jax.nn.sigmoid(gate_b) * y_b
        x = x + merged @ w_out[i].astype(x.dtype)

        h = rms_norm(x, mlp_norm[i])
        x = x + jnp.square(jax.nn.relu(h @ w_up[i].astype(x.dtype))) @ w_down[i].astype(x.dtype)

        gate = jax.nn.sigmoid(rms_norm(x, ple_norm[i]) @ w_ple_gate[i].astype(x.dtype))
        x = x + gate * (p[i].astype(x.dtype) @ w_ple[i].astype(x.dtype))
    return x
```

```python
import contextlib
import math
import numpy as np
import concourse.bass as bass
import concourse.mybir as mybir
from concourse.bass_utils import run_bass_kernel_spmd

F32 = mybir.dt.float32
BF16 = mybir.dt.bfloat16
I32 = mybir.dt.int32
U8 = mybir.dt.uint8
AF = mybir.ActivationFunctionType
ALU = mybir.AluOpType
AX = mybir.AxisListType

NDS = 28
NDS_SP = 20
EPS = 1e-6
PI = math.pi


class Prog:
    ENGS = ("pe", "act", "dve", "pool", "sp")

    def __init__(self):
        self.streams = {e: [] for e in self.ENGS}
        self.cnt = {e: 0 for e in self.ENGS}
        self.waited = {e: {} for e in self.ENGS}
        self.lastw = {}
        self.readers = {}
        self.ndma = 0
        self.nq = {}
        self.slotval = {}

    def _deps(self, reads, writes):
        toks = []
        for k in reads:
            t = self.lastw.get(k)
            if t is not None:
                toks.append(t)
        for k in writes:
            t = self.lastw.get(k)
            if t is not None:
                toks.append(t)
            toks.extend(self.readers.get(k, ()))
        return toks

    def _commit(self, tok, reads, writes):
        for k in reads:
            self.readers.setdefault(k, []).append(tok)
        for k in writes:
            self.lastw[k] = tok
            self.readers[k] = []

    def _need(self, eng, toks):
        best = {}
        for sk, val in toks:
            if sk == "pe" and eng == "pe":
                continue
            if val > best.get(sk, 0):
                best[sk] = val
        out = []
        w = self.waited[eng]
        for sk, val in best.items():
            if w.get(sk, 0) >= val:
                continue
            w[sk] = val
            out.append((sk, val))
        return out

    def op(self, eng, fn, reads=(), writes=()):
        pr = [k for k in reads if k.startswith("pb")]
        if pr:
            reads = [k for k in reads if not k.startswith("pb")]
            writes = list(writes) + pr
        toks = self._deps(reads, writes)
        waits = self._need(eng, toks)
        self.cnt[eng] += 1
        tok = (eng, self.cnt[eng])
        self.streams[eng].append((waits, fn, None))
        self._commit(tok, reads, writes)
        return tok

    def dma(self, q, fn, reads=(), writes=()):
        lo, n = (0, NDS_SP) if q == "sp" else (NDS_SP, NDS - NDS_SP)
        k = self.nq.get(q, 0)
        self.nq[q] = k + 1
        slot = lo + k % n
        val = 16 * (k // n + 1)
        self.ndma += 1
        self.slotval[slot] = val
        toks = self._deps(reads, writes)
        if val > 16:
            toks.append((("d", slot), val - 16))
        waits = self._need(q, toks)
        tok = (("d", slot), val)
        self.streams[q].append((waits, fn, slot))
        self._commit(tok, reads, writes)
        return tok

    def _all_toks(self):
        toks = [(e, self.cnt[e]) for e in self.ENGS if self.cnt[e] > 0]
        for s, v in self.slotval.items():
            toks.append((("d", s), v))
        return toks

    def barrier(self):
        toks = self._all_toks()
        for e in self.ENGS:
            waits = self._need(e, toks)
            if waits:
                self.streams[e].append((waits, None, None))
        self.lastw = {}
        self.readers = {}

    def sync_engines(self):
        toks = [(e, self.cnt[e]) for e in self.ENGS if self.cnt[e] > 0]
        for e in self.ENGS:
            waits = self._need(e, toks)
            if waits:
                self.streams[e].append((waits, None, None))
        self.lastw = {k: t for k, t in self.lastw.items() if not isinstance(t[0], str)}
        self.readers = {k: [t for t in v if not isinstance(t[0], str)] for k, v in self.readers.items()}

    def final_wait(self, eng="sp"):
        waits = self._need(eng, self._all_toks())
        self.streams[eng].append((waits, None, None))

    def emit(self, nc, ctx):
        esem = {e: ctx.enter_context(nc.semaphore("s_" + e)) for e in self.ENGS}
        dsem = [ctx.enter_context(nc.semaphore("d_%d" % i)) for i in range(NDS)]

        def semof(sk):
            return esem[sk] if isinstance(sk, str) else dsem[sk[1]]

        block = ctx.enter_context(nc.Block())

        def run(e, eng):
            for waits, fn, slot in self.streams[e]:
                for sk, val in waits:
                    eng.wait_ge(semof(sk), val)
                if fn is None:
                    continue
                ins = fn(eng)
                if slot is None:
                    ins.then_inc(esem[e], 1)
                else:
                    ins.then_inc(dsem[slot], 16)

        @block.tensor
        def _(eng):
            run("pe", eng)

        @block.scalar
        def _(eng):
            run("act", eng)

        @block.vector
        def _(eng):
            run("dve", eng)

        @block.gpsimd
        def _(eng):
            run("pool", eng)

        @block.sync
        def _(eng):
            run("sp", eng)


def _layout(items):
    off = {}
    o = 0
    for name, w in items:
        off[name] = (o, w)
        o += w
    return off, o


CST_ITEMS = [("ident", 128), ("tri", 128), ("mincl", 128), ("mstrict", 128), ("sellast", 128),
             ("sel63", 128), ("sel127", 128), ("invf", 8)]
CST, NCST = _layout(CST_ITEMS)
PV_ITEMS = [("attn_g", 8), ("mlp_g", 8), ("ple_g", 8), ("convw", 48), ("gdn_g", 128), ("q_g", 64), ("k_g", 64),
            ("lq1", 64), ("lk1", 64), ("lq2", 64), ("lk2", 64), ("diff_g", 1), ("a_log", 4), ("dt_b", 4)]
PV, NPV = _layout(PV_ITEMS)

C_AQ, C_AK, C_AV, C_AZ, C_AA, C_AB, C_BQ, C_BK, C_BV, C_GA, C_GB, C_END = (
    0, 512, 1024, 1536, 2048, 2052, 2056, 2568, 3080, 3592, 4616, 5640)


def host_consts():
    c = np.zeros((128, NCST), np.float32)
    idx = np.arange(128)
    same = (idx[:, None] // 64) == (idx[None, :] // 64)

    def put(name, a):
        o, w = CST[name]
        c[:, o:o + w] = a.reshape(128, w)

    put("ident", np.eye(128, dtype=np.float32))
    put("tri", (same & (idx[:, None] <= idx[None, :])).astype(np.float32))
    put("mincl", (same & (idx[:, None] >= idx[None, :])).astype(np.float32))
    put("mstrict", (same & (idx[:, None] > idx[None, :])).astype(np.float32))
    put("sellast", (idx[:, None] == (64 * (idx[None, :] // 64) + 63)).astype(np.float32))
    put("sel63", np.broadcast_to((idx[:, None] == 63), (128, 128)).astype(np.float32))
    put("sel127", np.broadcast_to((idx[:, None] == 127), (128, 128)).astype(np.float32))
    rot = 16
    invf = (500000.0 ** (-np.arange(0, rot, 2, dtype=np.float32) / rot)).astype(np.float32)
    put("invf", np.broadcast_to(invf[None, :], (128, 8)))
    return c


def host_cmask():
    idx = np.arange(128)
    q = np.arange(512)
    cm = np.stack([((128 * m + idx[:, None]) <= q[None, :]).astype(np.float32) for m in range(4)], 1)
    return np.ascontiguousarray(cm.reshape(128, 2048))


def host_pvec(inp):
    v = np.zeros((128, NPV), np.float32)

    def put(name, a):
        o, w = PV[name]
        v[:, o:o + w] = np.asarray(a, np.float32).reshape(128, w)

    def pc(g):
        return np.ascontiguousarray(np.asarray(g, np.float32).reshape(8, 128).T)

    def bc(g):
        g = np.asarray(g, np.float32).reshape(1, -1)
        return np.broadcast_to(g, (128, g.shape[1]))

    put("attn_g", pc(inp["attn_norm"][0]))
    put("mlp_g", pc(inp["mlp_norm"][0]))
    put("ple_g", pc(inp["ple_norm"][0]))
    cw = np.asarray(inp["conv_w"][0], np.float32)
    put("convw", np.ascontiguousarray(cw.reshape(4, 12, 128).transpose(2, 1, 0)))
    put("gdn_g", bc(inp["gdn_norm"][0]))
    put("q_g", bc(inp["q_norm"][0]))
    put("k_g", bc(inp["k_norm"][0]))
    put("lq1", bc(inp["lambda_q1"][0]))
    put("lk1", bc(inp["lambda_k1"][0]))
    put("lq2", bc(inp["lambda_q2"][0]))
    put("lk2", bc(inp["lambda_k2"][0]))
    put("diff_g", np.asarray(inp["diff_norm"][0], np.float32).reshape(128, 1))
    put("a_log", bc(inp["a_log"][0]))
    put("dt_b", bc(inp["dt_bias"][0]))
    return v


def build(TC, TO, debug=False, stage=99):
    NTOK = TC + TO
    NT = NTOK // 128
    NTC = TC // 128
    NQG = TO // 512
    NG = NTOK // 512
    nc = bass.Bass("TRN2", target_bir_lowering=False)

    def din(name, shape, dt=F32):
        return nc.dram_tensor(name, shape, dt, kind="ExternalInput").ap()

    xin = din("xin", [NTOK, 1024])
    pin = din("pin", [TO, 256])
    pos_d = din("pos", [128, NT], I32)
    ctxm_d = din("ctxm", [128, 1])
    cst_d = din("cst", [128, NCST])
    cmask_d = din("cmask", [128, 2048])
    pv_d = din("pv", [128, NPV])
    w_in = din("w_in", [1024, 5640])
    w_o_a = din("w_o_a", [512, 1024])
    w_o_b = din("w_o_b", [512, 1024])
    w_out = din("w_out", [1024, 1024])
    w_up = din("w_up", [1024, 4096])
    w_down = din("w_down", [4096, 1024])
    w_pg = din("w_pg", [1024, 1024])
    w_ple = din("w_ple", [256, 1024])
    out_d = nc.dram_tensor("out", [TO, 1024], F32, kind="ExternalOutput").ap()

    def dscr(name, shape, dt=BF16):
        return nc.dram_tensor(name, shape, dt).ap()

    win_b = dscr("win_b", [1024, 5640])
    woa_b = dscr("woa_b", [512, 1024])
    wob_b = dscr("wob_b", [512, 1024])
    wout_b = dscr("wout_b", [1024, 1024])
    wup_b = dscr("wup_b", [1024, 4096])
    wdn_b = dscr("wdn_b", [4096, 1024])
    wpg_b = dscr("wpg_b", [1024, 1024])
    wple_b = dscr("wple_b", [256, 1024])
    obT_d = dscr("obT_d", [512, TO])
    oaT_d = dscr("oaT_d", [512, TO])
    dbg = {}
    if debug:
        dbg["obT"] = nc.dram_tensor("dbg_obT", [512, TO], F32, kind="ExternalOutput").ap()
        dbg["oaT"] = nc.dram_tensor("dbg_oaT", [512, TO], F32, kind="ExternalOutput").ap()

    P = Prog()
    ctx = contextlib.ExitStack()
    with ctx:
        def sb(name, shape, dt=F32):
            return ctx.enter_context(nc.sbuf_tensor(name, shape, dt))

        cs = sb("cs", [128, NCST])
        pvs = sb("pvs", [128, NPV])

        def C(name):
            o, w = CST[name]
            return cs[:, o:o + w]

        def PVv(name):
            o, w = PV[name]
            return pvs[:, o:o + w]

        identb = sb("identb", [128, 128], BF16)
        onesb = sb("onesb", [128, 128], BF16)
        onesf = sb("onesf", [128, 128])
        cmaskb = sb("cmaskb", [128, 4, 512], BF16)
        epsT = sb("epsT", [128, 1])
        oneT = sb("oneT", [128, 1])
        nhalf = sb("nhalf", [128, 1])
        phalf = sb("phalf", [128, 1])
        posi = sb("posi", [128, NT], I32)
        posf = sb("posf", [128, NT])
        sinT = sb("sinT", [128, NT, 8])
        cosT = sb("cosT", [128, NT, 8])
        ctxm = sb("ctxm_s", [128, 1])
        negshift = sb("negshift", [128, 1])
        biasC = sb("biasC", [128, 1])
        neglam = sb("neglam", [128, 1])
        gscale = sb("gscale", [128, 1])
        negA = sb("negA", [128, 4])
        tmp64 = sb("tmp64", [128, 64])
        tmp1 = sb("tmp1", [128, 4])
        pb = [ctx.enter_context(nc.psum_tensor("pb%d" % i, [128, 512], F32)) for i in range(8)]

        def pbf(i):
            return pb[i][:, :].bitcast(BF16)

        ARENA = 188 * 1024
        arena = sb("arena", [128, ARENA], U8)
        apos = [0]

        def areset():
            apos[0] = 0

        def A(shape, dt=F32):
            esz = {F32: 4, BF16: 2, I32: 4}[dt]
            n = int(np.prod(shape[1:])) * esz
            o = apos[0]
            apos[0] = (o + n + 63) // 64 * 64
            assert apos[0] <= ARENA, ("arena overflow", apos[0])
            v = arena[:, o:o + n].bitcast(dt)
            if len(shape) == 3:
                v = v.rearrange("p (a b) -> p a b", a=shape[1])
            elif len(shape) == 4:
                v = v.rearrange("p (a b c) -> p a b c", a=shape[1], b=shape[2])
            return v

        def OP(eng, name, reads, writes, *args, **kw):
            P.op(eng, lambda e: getattr(e, name)(*args, **kw), reads, writes)

        def DMA(q, out, in_, reads=(), writes=()):
            P.dma(q, lambda e: e.dma_start(out=out, in_=in_), reads, writes)

        apos[0] = 180 * 1024
        angT = A([128, NT, 8])
        kfT = A([128, NT, 8])
        kiT = A([128, NT, 8], I32)
        DMA("sp", cs[:], cst_d[:, :], writes=["cs"])
        DMA("sp", pvs[:], pv_d[:, :], writes=["pvs"])
        DMA("sp", posi[:], pos_d[:, :], writes=["posi"])
        DMA("sp", ctxm[:], ctxm_d[:, :], writes=["ctxm"])
        for r in range(2):
            DMA("pool", win_b[r * 512:(r + 1) * 512, C_BK:C_BV + 512], w_in[r * 512:(r + 1) * 512, C_BK:C_BV + 512], writes=["win_kv"])
        for r in range(8):
            DMA("pool", win_b[r * 128:(r + 1) * 128, 0:C_BK], w_in[r * 128:(r + 1) * 128, 0:C_BK], writes=["win_b%d" % r])
        for r in range(8):
            DMA("pool", win_b[r * 128:(r + 1) * 128, C_GA:C_END], w_in[r * 128:(r + 1) * 128, C_GA:C_END], writes=["win_c%d" % r])
        WIN = ["win_b%d" % r for r in range(8)] + ["win_c%d" % r for r in range(8)]
        DMA("pool", woa_b[:, :], w_o_a[:, :], writes=["woa_b"])
        DMA("pool", wob_b[:, :], w_o_b[:, :], writes=["wob_b"])
        for r in range(2):
            DMA("pool", wout_b[r * 512:(r + 1) * 512, :], w_out[r * 512:(r + 1) * 512, :], writes=["wout_b%d" % r])
            DMA("pool", wpg_b[r * 512:(r + 1) * 512, :], w_pg[r * 512:(r + 1) * 512, :], writes=["wpg_b%d" % r])
        for r in range(8):
            DMA("pool", wup_b[r * 128:(r + 1) * 128, :], w_up[r * 128:(r + 1) * 128, :], writes=["wup_b%d" % r])
        for r in range(8):
            DMA("pool", wdn_b[r * 512:(r + 1) * 512, :], w_down[r * 512:(r + 1) * 512, :], writes=["wdn_b%d" % r])
        DMA("pool", wple_b[:, :], w_ple[:, :], writes=["wple_b"])

        OP("dve", "tensor_copy", ["cs"], ["identb"], out=identb[:], in_=C("ident"))
        OP("pool", "memset", [], ["onesb"], onesb[:], 1.0)
        OP("pool", "memset", [], ["onesf"], onesf[:], 1.0)
        OP("pool", "memset", [], ["epsT"], epsT[:], EPS)
        OP("pool", "memset", [], ["oneT"], oneT[:], 1.0)
        OP("pool", "memset", [], ["nhalf"], nhalf[:], -0.5)
        OP("pool", "memset", [], ["phalf"], phalf[:], 0.5)
        DMA("pool", cmaskb[:].rearrange("p a b -> p (a b)"), cmask_d[:, :], writes=["cmaskb"])
        OP("dve", "tensor_copy", ["posi"], ["posf"], out=posf[:], in_=posi[:])
        OP("dve", "tensor_tensor", ["posf", "cs"], ["angT"], out=angT[:],
           in0=posf[:, :].unsqueeze(2).to_broadcast([128, NT, 8]),
           in1=C("invf").unsqueeze(1).to_broadcast([128, NT, 8]), op=ALU.mult)

        def sin_table(dst, shift, key):
            if shift != 0.0:
                OP("dve", "tensor_scalar", ["angT"], ["angS"], out=dst[:], in0=angT[:], scalar1=shift, scalar2=None, op0=ALU.add)
                src, skey = dst, "angS"
            else:
                src, skey = angT, "angT"
            OP("dve", "tensor_scalar", [skey], ["kfT"], out=kfT[:], in0=src[:], scalar1=1.0 / (2 * PI), scalar2=None, op0=ALU.mult)
            OP("dve", "tensor_copy", ["kfT"], ["kiT"], out=kiT[:], in_=kfT[:])
            OP("dve", "tensor_copy", ["kiT"], ["kfT"], out=kfT[:], in_=kiT[:])
            OP("dve", "scalar_tensor_tensor", ["kfT", skey], ["angR"], out=kfT[:], in0=kfT[:], scalar=-2 * PI, in1=src[:], op0=ALU.mult, op1=ALU.add)
            OP("dve", "tensor_scalar", ["angR"], ["angR"], out=kfT[:], in0=kfT[:], scalar1=3.1415925, scalar2=-3.1415925, op0=ALU.min, op1=ALU.max)
            OP("act", "activation", ["angR"], [key, "angS", "kfT"], out=dst[:], in_=kfT[:], func=AF.Sin)

        sin_table(cosT, PI / 2, "cosT")
        sin_table(sinT, 0.0, "sinT")
        OP("dve", "tensor_tensor", ["pvs"], ["tmp64"], out=tmp64[:], in0=PVv("q_g"), in1=PVv("q_g"), op=ALU.mult)
        OP("dve", "tensor_reduce", ["tmp64"], ["tmp1a"], out=tmp1[:, 0:1], in_=tmp64[:], axis=AX.X, op=ALU.max)
        OP("dve", "tensor_tensor", ["pvs", "tmp1a"], ["tmp64"], out=tmp64[:], in0=PVv("k_g"), in1=PVv("k_g"), op=ALU.mult)
        OP("dve", "tensor_reduce", ["tmp64"], ["tmp1b"], out=tmp1[:, 1:2], in_=tmp64[:], axis=AX.X, op=ALU.max)
        OP("dve", "tensor_tensor", ["tmp1a", "tmp1b"], ["negshift"], out=negshift[:], in0=tmp1[:, 0:1], in1=tmp1[:, 1:2], op=ALU.mult)
        OP("pool", "tensor_tensor", ["negshift", "phalf"], ["negshift"], out=negshift[:], in0=negshift[:], in1=phalf[:], op=ALU.pow)
        OP("dve", "tensor_scalar", ["negshift"], ["negshift"], out=negshift[:], in0=negshift[:], scalar1=-8.0, scalar2=None, op0=ALU.mult)
        OP("dve", "tensor_tensor", ["negshift", "ctxm"], ["biasC"], out=biasC[:], in0=negshift[:], in1=ctxm[:], op=ALU.add)
        OP("dve", "tensor_tensor", ["pvs", "tmp1b"], ["tmp64"], out=tmp64[:], in0=PVv("lq1"), in1=PVv("lk1"), op=ALU.mult)
        OP("dve", "tensor_reduce", ["tmp64"], ["tmp1c"], out=tmp1[:, 2:3], in_=tmp64[:], axis=AX.X, op=ALU.add)
        OP("dve", "tensor_tensor", ["pvs", "tmp1c"], ["tmp64"], out=tmp64[:], in0=PVv("lq2"), in1=PVv("lk2"), op=ALU.mult)
        OP("dve", "tensor_reduce", ["tmp64"], ["tmp1d"], out=tmp1[:, 3:4], in_=tmp64[:], axis=AX.X, op=ALU.add)
        OP("act", "activation", ["tmp1c", "tmp1d"], ["tmp1e"], out=tmp1[:, 2:4], in_=tmp1[:, 2:4], func=AF.Exp)
        OP("dve", "tensor_tensor", ["tmp1e"], ["neglam"], out=neglam[:], in0=tmp1[:, 3:4], in1=tmp1[:, 2:3], op=ALU.subtract)
        lam_init = 0.8 - 0.6 * math.exp(-0.3 * 0)
        OP("dve", "tensor_scalar", ["neglam"], ["neglam"], out=neglam[:], in0=neglam[:], scalar1=-lam_init, scalar2=None, op0=ALU.add)
        OP("dve", "tensor_scalar", ["pvs"], ["gscale"], out=gscale[:], in0=PVv("diff_g"), scalar1=1.0 - lam_init, scalar2=None, op0=ALU.mult)
        OP("act", "activation", ["pvs"], ["negA"], out=negA[:], in_=PVv("a_log"), func=AF.Exp)
        OP("dve", "tensor_scalar", ["negA"], ["negA"], out=negA[:], in0=negA[:], scalar1=-1.0, scalar2=None, op0=ALU.mult)

        P.sync_engines()

        state = {"xi": 0}

        def load_norm_T(x_rows, gname, hT_dst, hkey, xt_bufs, junk, xn, ss, keep_x=None, junk_key="junk"):
            if keep_x is None:
                i = state["xi"] % len(xt_bufs)
                state["xi"] += 1
                xt, xk = xt_bufs[i], "xt%d" % i
                DMA("sp", xt, x_rows, writes=[xk])
            else:
                xt, xk = keep_x
            OP("act", "activation", [xk], [junk_key, "ss"], out=junk, in_=xt, func=AF.Square, accum_out=ss[:, 0:1])
            OP("dve", "tensor_scalar", ["ss"], ["ss"], out=ss[:, 0:1], in0=ss[:, 0:1], scalar1=1.0 / 1024, scalar2=EPS, op0=ALU.mult, op1=ALU.add)
            OP("act", "activation", ["ss"], ["ss"], out=ss[:, 0:1], in_=ss[:, 0:1], func=AF.Ln)
            OP("act", "activation", ["ss"], ["ss"], out=ss[:, 0:1], in_=ss[:, 0:1], func=AF.Exp, scale=-0.5)
            OP("dve", "tensor_scalar", [xk, "ss"], ["xn"], out=xn, in0=xt, scalar1=ss[:, 0:1], scalar2=None, op0=ALU.mult)
            tp = pbf(6)
            for c in range(8):
                OP("pe", "transpose", ["xn"], ["pb6"], out=tp[:, c * 128:(c + 1) * 128], in_=xn[:, c * 128:(c + 1) * 128], identity=identb[:])
            OP("dve", "tensor_tensor", ["pb6", "pvs"], [hkey], out=hT_dst,
               in0=tp.rearrange("p (c t) -> p c t", c=8),
               in1=PVv(gname).unsqueeze(2).to_broadcast([128, 8, 128]), op=ALU.mult)

        def qk_post(bank, gname, t, dest, dkey, W):
            kf, ksq, ssk, kbb, rt = W["kf"], W["ksq"], W["ssk"], W["kbb"], W["rt"]
            bk = "pb%d" % bank
            OP("act", "activation", [bk], ["kf"], out=kf, in_=pb[bank][:, :], func=AF.Copy)
            OP("pool", "tensor_tensor", ["kf"], ["ksq"], out=ksq, in0=kf, in1=kf, op=ALU.mult)
            OP("dve", "tensor_reduce", ["ksq"], ["ssk"], out=ssk, in_=ksq.rearrange("p (g d) -> p g d", g=8), axis=AX.X, op=ALU.add)
            OP("dve", "tensor_scalar", ["ssk"], ["ssk"], out=ssk, in0=ssk, scalar1=1.0 / 64, scalar2=EPS, op0=ALU.mult, op1=ALU.add)
            OP("act", "activation", ["ssk"], ["ssk"], out=ssk, in_=ssk, func=AF.Ln)
            OP("act", "activation", ["ssk"], ["ssk"], out=ssk, in_=ssk, func=AF.Exp, scale=-0.5)
            kf3 = kf.rearrange("p (g d) -> p g d", g=8)
            OP("dve", "tensor_tensor", ["kf", "ssk"], ["kf"], out=kf3, in0=kf3, in1=ssk.unsqueeze(2).to_broadcast([128, 8, 64]), op=ALU.mult)
            OP("dve", "tensor_tensor", ["kf", "pvs"], ["kf"], out=kf3, in0=kf3, in1=PVv(gname).unsqueeze(1).to_broadcast([128, 8, 64]), op=ALU.mult)
            kb3 = kbb.rearrange("p (g d) -> p g d", g=8)
            OP("act", "activation", ["kf"], ["kbb"], out=kbb, in_=kf, func=AF.Copy)
            cb = cosT[:, t, :].unsqueeze(1).to_broadcast([128, 8, 8])
            sn = sinT[:, t, :].unsqueeze(1).to_broadcast([128, 8, 8])
            x1, x2 = kf3[:, :, 0:8], kf3[:, :, 8:16]
            OP("dve", "tensor_tensor", ["kf", "cosT"], ["rt0"], out=rt[:, 0], in0=x1, in1=cb, op=ALU.mult)
            OP("dve", "tensor_tensor", ["kf", "sinT"], ["rt1"], out=rt[:, 1], in0=x2, in1=sn, op=ALU.mult)
            OP("dve", "tensor_tensor", ["kf", "cosT"], ["rt2"], out=rt[:, 2], in0=x2, in1=cb, op=ALU.mult)
            OP("dve", "tensor_tensor", ["kf", "sinT"], ["rt3"], out=rt[:, 3], in0=x1, in1=sn, op=ALU.mult)
            OP("dve", "tensor_tensor", ["rt0", "rt1"], ["kbb"], out=kb3[:, :, 0:8], in0=rt[:, 0], in1=rt[:, 1], op=ALU.subtract)
            OP("dve", "tensor_tensor", ["rt2", "rt3"], ["kbb"], out=kb3[:, :, 8:16], in0=rt[:, 2], in1=rt[:, 3], op=ALU.add)
            tp = pbf(6)
            for h in range(4):
                OP("pe", "transpose", ["kbb"], ["pb6"], out=tp[:, h * 128:(h + 1) * 128], in_=kbb[:, h * 128:(h + 1) * 128], identity=identb[:])
            OP("act", "activation", ["pb6"], [dkey], out=dest, in_=tp[:, 0:512].rearrange("p (h t) -> p h t", h=4), func=AF.Copy)

        areset()
        KT = A([128, 4, NTOK], BF16)
        Vt = A([128, NT, 512], BF16)

        def p1set():
            return {"xt": A([128, 1024]), "junk": A([128, 1024], BF16), "xn": A([128, 1024], BF16), "hT": A([128, 8, 128], BF16),
                    "ss": A([128, 1]), "kf": A([128, 512]), "ksq": A([128, 512]), "ssk": A([128, 8]), "kbb": A([128, 512], BF16),
                    "rt": A([128, 4, 8, 8])}
        set0 = p1set()
        xtb = [set0["xt"]]
        junk, xn, hT1, ssx = set0["junk"], set0["xn"], set0["hT"], set0["ss"]
        Wqk = set0
        mark_kv = apos[0]
        Wkv = A([128, 8, 1024], BF16)
        set1 = p1set()
        winv = win_b.rearrange("(c p) n -> p c n", p=128)
        DMA("sp", Wkv[:, :, 0:512], winv[:, :, C_BK:C_BK + 512], reads=["win_kv"], writes=["Wkv"])
        DMA("sp", Wkv[:, :, 512:1024], winv[:, :, C_BV:C_BV + 512], reads=["win_kv"], writes=["Wkv"])

        def p1_tile(t, cs, qdest=None):
            B_ = set0 if cs == 0 else set1
            tb, kb_, vb_ = (6, 7, 5) if cs == 0 else (3, 4, 2)
            if qdest is not None:
                tb, kb_, vb_ = 6, 6, None
            K_ = lambda n: "p1%s_%d" % (n, cs)
            xt, ss = B_["xt"], B_["ss"]
            DMA("sp", xt, xin[t * 128:(t + 1) * 128, :], writes=[K_("xt")])
            yield
            OP("act", "activation", [K_("xt")], [K_("junk"), K_("ss")], out=B_["junk"], in_=xt, func=AF.Square, accum_out=ss[:, 0:1])
            yield
            OP("dve", "tensor_scalar", [K_("ss")], [K_("ss")], out=ss[:, 0:1], in0=ss[:, 0:1], scalar1=1.0 / 1024, scalar2=EPS, op0=ALU.mult, op1=ALU.add)
            yield
            OP("act", "activation", [K_("ss")], [K_("ss")], out=ss[:, 0:1], in_=ss[:, 0:1], func=AF.Ln)
            yield
            OP("act", "activation", [K_("ss")], [K_("ss")], out=ss[:, 0:1], in_=ss[:, 0:1], func=AF.Exp, scale=-0.5)
            yield
            OP("dve", "tensor_scalar", [K_("xt"), K_("ss")], [K_("xn")], out=B_["xn"], in0=xt, scalar1=ss[:, 0:1], scalar2=None, op0=ALU.mult)
            yield
            tp = pbf(tb)
            for c in range(8):
                OP("pe", "transpose", [K_("xn")], ["pb%d" % tb], out=tp[:, c * 128:(c + 1) * 128], in_=B_["xn"][:, c * 128:(c + 1) * 128], identity=identb[:])
            yield
            OP("dve", "tensor_tensor", ["pb%d" % tb, "pvs"], [K_("hT")], out=B_["hT"], in0=tp.rearrange("p (c t) -> p c t", c=8),
               in1=PVv("attn_g").unsqueeze(2).to_broadcast([128, 8, 128]), op=ALU.mult)
            yield
            if qdest is None:
                for c in range(8):
                    OP("pe", "matmul", [K_("hT"), "Wkv"], ["pb%d" % kb_], pb[kb_][:, :], lhsT=B_["hT"][:, c, :], rhs=Wkv[:, c, 0:512], start=(c == 0), stop=(c == 7))
                for c in range(8):
                    OP("pe", "matmul", [K_("hT"), "Wkv"], ["pb%d" % vb_], pb[vb_][:, :], lhsT=B_["hT"][:, c, :], rhs=Wkv[:, c, 512:1024], start=(c == 0), stop=(c == 7))
            else:
                for c in range(8):
                    OP("pe", "matmul", [K_("hT"), "Wq"], ["pb%d" % kb_], pb[kb_][:, :], lhsT=B_["hT"][:, c, :], rhs=qdest[2][:, c, :], start=(c == 0), stop=(c == 7))
            yield
            kf, ksq, ssk, kbb, rt = B_["kf"], B_["ksq"], B_["ssk"], B_["kbb"], B_["rt"]
            OP("act", "activation", ["pb%d" % kb_], [K_("kf")], out=kf, in_=pb[kb_][:, :], func=AF.Copy)
            yield
            if qdest is None:
                OP("act", "activation", ["pb%d" % vb_], ["V%d" % t], out=Vt[:, t, :], in_=pb[vb_][:, :], func=AF.Copy)
            OP("pool", "tensor_tensor", [K_("kf")], [K_("ksq")], out=ksq, in0=kf, in1=kf, op=ALU.mult)
            yield
            OP("dve", "tensor_reduce", [K_("ksq")], [K_("ssk")], out=ssk, in_=ksq.rearrange("p (g d) -> p g d", g=8), axis=AX.X, op=ALU.add)
            yield
            OP("dve", "tensor_scalar", [K_("ssk")], [K_("ssk")], out=ssk, in0=ssk, scalar1=1.0 / 64, scalar2=EPS, op0=ALU.mult, op1=ALU.add)
            yield
            OP("act", "activation", [K_("ssk")], [K_("ssk")], out=ssk, in_=ssk, func=AF.Ln)
            yield
            OP("act", "activation", [K_("ssk")], [K_("ssk")], out=ssk, in_=ssk, func=AF.Exp, scale=-0.5)
            yield
            kf3 = kf.rearrange("p (g d) -> p g d", g=8)
            OP("dve", "tensor_tensor", [K_("kf"), K_("ssk")], [K_("kf")], out=kf3, in0=kf3, in1=ssk.unsqueeze(2).to_broadcast([128, 8, 64]), op=ALU.mult)
            yield
            OP("dve", "tensor_tensor", [K_("kf"), "pvs"], [K_("kf")], out=kf3, in0=kf3, in1=PVv("k_g" if qdest is None else "q_g").unsqueeze(1).to_broadcast([128, 8, 64]), op=ALU.mult)
            yield
            kb3 = kbb.rearrange("p (g d) -> p g d", g=8)
            OP("act", "activation", [K_("kf")], [K_("kbb")], out=kbb, in_=kf, func=AF.Copy)
            cb = cosT[:, t, :].unsqueeze(1).to_broadcast([128, 8, 8])
            sn = sinT[:, t, :].unsqueeze(1).to_broadcast([128, 8, 8])
            x1, x2 = kf3[:, :, 0:8], kf3[:, :, 8:16]
            OP("dve", "tensor_tensor", [K_("kf")], [K_("rt0")], out=rt[:, 0], in0=x1, in1=cb, op=ALU.mult)
            OP("dve", "tensor_tensor", [K_("kf")], [K_("rt1")], out=rt[:, 1], in0=x2, in1=sn, op=ALU.mult)
            yield
            OP("dve", "tensor_tensor", [K_("kf")], [K_("rt2")], out=rt[:, 2], in0=x2, in1=cb, op=ALU.mult)
            OP("dve", "tensor_tensor", [K_("kf")], [K_("rt3")], out=rt[:, 3], in0=x1, in1=sn, op=ALU.mult)
            yield
            OP("dve", "tensor_tensor", [K_("rt0"), K_("rt1")], [K_("kbb")], out=kb3[:, :, 0:8], in0=rt[:, 0], in1=rt[:, 1], op=ALU.subtract)
            OP("dve", "tensor_tensor", [K_("rt2"), K_("rt3")], [K_("kbb")], out=kb3[:, :, 8:16], in0=rt[:, 2], in1=rt[:, 3], op=ALU.add)
            yield
            for h in range(4):
                OP("pe", "transpose", [K_("kbb")], ["pb%d" % tb], out=tp[:, h * 128:(h + 1) * 128], in_=kbb[:, h * 128:(h + 1) * 128], identity=identb[:])
            yield
            if qdest is None:
                OP("act", "activation", ["pb%d" % tb], ["KT%d" % t], out=KT[:, :, t * 128:(t + 1) * 128], in_=tp[:, 0:512].rearrange("p (h t) -> p h t", h=4), func=AF.Copy)
            else:
                OP("act", "activation", ["pb%d" % tb], [qdest[1]], out=qdest[0], in_=tp[:, 0:512].rearrange("p (h t) -> p h t", h=4), func=AF.Copy)
            yield

        def zip_run(*gens):
            gens = list(gens)
            while gens:
                for gnr in list(gens):
                    try:
                        next(gnr)
                    except StopIteration:
                        gens.remove(gnr)

        for t in range(0, NT, 2):
            zip_run(p1_tile(t, 0), p1_tile(t + 1, 1))

        P.barrier()
        apos[0] = mark_kv
        Wq = A([128, 8, 512], BF16)
        QT2 = [A([128, 4, 512], BF16), A([128, 4, 512], BF16)]
        ptp = [A([128, 2, 512], BF16) for _ in range(3)]
        o0 = A([128, 512])
        o1 = A([128, 512])
        rl = A([128, 512])
        osq = A([128, 512])
        obt = [A([128, 512], BF16) for _ in range(2)]
        lacc2 = A([128, 2, 512])
        DMA("sp", Wq, winv[:, :, C_BQ:C_BQ + 512], reads=WIN, writes=["Wq"])
        kvkeys_k = ["KT%d" % t for t in range(NT)]
        spi = 0
        pti = 0
        obi = 0
        def q_setup(g):
            for i in range(4):
                t = NTC + 4 * g + i
                yield from p1_tile(t, 0, qdest=(QT2[g % 2][:, :, i * 128:(i + 1) * 128], "QT%d_%d" % (g % 2, i), Wq))

        for _ in q_setup(0):
            pass
        for g in range(NQG):
            QT = QT2[g % 2]
            qgen = q_setup(g + 1) if g + 1 < NQG else iter(())
            qkeys = ["QT%d_%d" % (g % 2, i) for i in range(4)]
            J = NTC + 4 * g + 4
            tiles = [(h, j) for h in range(4) for j in range(J)]
            tinfo = {}

            def emit_qk(n):
                nonlocal spi, pti
                h, j = tiles[n]
                banks = (0, 1) if spi % 2 == 0 else (4, 5)
                spi += 1
                for c in range(2):
                    OP("pe", "matmul", qkeys + ["KT%d" % j], ["pb%d" % banks[c]], pb[banks[c]][:, :],
                       lhsT=KT[64 * c:64 * c + 64, h, j * 128:(j + 1) * 128], rhs=QT[64 * c:64 * c + 64, h, :], start=True, stop=True)
                pi = pti % 3
                pti += 1
                for c in range(2):
                    pk = "pt%d_%d" % (pi, c)
                    bias = biasC if j < NTC else negshift
                    OP("act", "activation", ["pb%d" % banks[c], "biasC", "negshift"], [pk], out=ptp[pi][:, c, :], in_=pb[banks[c]][:, :], func=AF.Exp, bias=bias[:, 0:1], scale=0.125)
                    m = j - (NTC + 4 * g)
                    if m >= 0:
                        OP("pool", "tensor_tensor", [pk, "cmaskb"], [pk], out=ptp[pi][:, c, :], in0=ptp[pi][:, c, :], in1=cmaskb[:, m, :], op=ALU.mult)
                tinfo[n] = pi

            def emit_pv(n):
                h, j = tiles[n]
                pi = tinfo.pop(n)
                pks = ["pt%d_%d" % (pi, c) for c in range(2)]
                for c in range(2):
                    OP("pe", "matmul", [pks[c], "V%d" % j], ["pb%d" % (2 + c)], pb[2 + c][:, :], lhsT=Vt[:, j, h * 128:(h + 1) * 128], rhs=ptp[pi][:, c, :], start=(j == 0), stop=(j == J - 1))
                if j == 0:
                    OP("dve", "tensor_copy", pks, ["lacc"], out=lacc2, in_=ptp[pi])
                else:
                    OP("dve", "tensor_tensor", pks + ["lacc"], ["lacc"], out=lacc2, in0=lacc2, in1=ptp[pi], op=ALU.add)
                if j == J - 1:
                    emit_post(h)

            def emit_post(h):
                nonlocal obi
                OP("pe", "matmul", ["lacc"], ["pb7"], pb[7][:, :], lhsT=onesf[:], rhs=lacc2[:, 0, :], start=True, stop=True)
                OP("act", "activation", ["pb7"], ["rl"], out=rl, in_=pb[7][:, :], func=AF.Ln)
                OP("act", "activation", ["rl"], ["rl"], out=rl, in_=rl, func=AF.Exp, scale=-1.0)
                OP("pe", "matmul", ["lacc"], ["pb7"], pb[7][:, :], lhsT=onesf[:], rhs=lacc2[:, 1, :], start=True, stop=True)
                OP("dve", "tensor_tensor", ["pb2", "rl"], ["o0"], out=o0, in0=pb[2][:, :], in1=rl, op=ALU.mult)
                OP("act", "activation", ["pb7"], ["rl"], out=rl, in_=pb[7][:, :], func=AF.Ln)
                OP("act", "activation", ["rl"], ["rl"], out=rl, in_=rl, func=AF.Exp, scale=-1.0)
                OP("dve", "tensor_tensor", ["pb3", "rl"], ["o1"], out=o1, in0=pb[3][:, :], in1=rl, op=ALU.mult)
                OP("dve", "scalar_tensor_tensor", ["o0", "o1", "neglam"], ["o0"], out=o0, in0=o1, scalar=neglam[:, 0:1], in1=o0, op0=ALU.mult, op1=ALU.add)
                OP("pool", "tensor_tensor", ["o0"], ["osq"], out=osq, in0=o0, in1=o0, op=ALU.mult)
                pending.append((nnow[0] + 6, h))

            def emit_post_b(h):
                nonlocal obi
                OP("pe", "matmul", ["osq", "onesf"], ["pb7"], pb[7][:, :], lhsT=onesf[:], rhs=osq, start=True, stop=True)
                OP("dve", "tensor_scalar", ["pb7"], ["rl"], out=rl, in0=pb[7][:, :], scalar1=1.0 / 128, scalar2=EPS, op0=ALU.mult, op1=ALU.add)
                OP("act", "activation", ["rl"], ["rl"], out=rl, in_=rl, func=AF.Ln)
                OP("act", "activation", ["rl"], ["rl"], out=rl, in_=rl, func=AF.Exp, scale=-0.5)
                ob = obt[obi % 2]
                obk = "obt%d" % (obi % 2)
                obi += 1
                OP("dve", "scalar_tensor_tensor", ["o0", "gscale", "rl"], [obk], out=ob, in0=o0, scalar=gscale[:, 0:1], in1=rl, op0=ALU.mult, op1=ALU.mult)
                DMA("sp", obT_d[h * 128:(h + 1) * 128, g * 512:(g + 1) * 512], ob, reads=[obk], writes=["obT_d"])

            LA = 1
            pending = []
            nnow = [0]
            for n in range(len(tiles) + LA):
                nnow[0] = n
                if n < len(tiles):
                    emit_qk(n)
                if n - LA >= 0:
                    emit_pv(n - LA)
                next(qgen, None)
                while pending and pending[0][0] <= n:
                    emit_post_b(pending.pop(0)[1])
            while pending:
                emit_post_b(pending.pop(0)[1])
            for _ in qgen:
                pass

        P.barrier()
        areset()
        Wg = A([128, 8, 1536], BF16)
        Wz = A([128, 8, 512], BF16)
        Wab = A([128, 8, 8], BF16)
        xtb = [A([128, 1024])]
        sq = A([128, 512])
        junk = sq.bitcast(BF16)
        rn = A([128, 512])
        xn = A([128, 1024], BF16)
        ssx = A([128, 1])
        hT4 = A([128, 8, 512], BF16)
        pbuf = [A([128, 516]), A([128, 516])]
        halo = A([128, 12, 4])
        cv = A([128, 12, 512], BF16)
        qTg = A([128, 4, 512], BF16)
        kTg = A([128, 4, 512], BF16)
        vTg = A([128, 4, 512], BF16)
        S = A([128, 4, 128])
        Sb = A([128, 4, 128], BF16)
        SMN = ("xa", "gt", "beta", "nbeta", "Gc", "eG", "bg", "dgl", "ekd")
        F3N = ("diagG", "GSs", "EGb", "D", "Dm", "Ds", "L0")
        B3N = ("ktm", "vtm", "vb", "kbg", "X0", "X1", "Y0", "Y1", "R0", "R1", "Aa")
        smS = [{n: A([128, 4]) for n in SMN} for _ in range(2)]
        f3S = [{n: A([128, 4, 128]) for n in F3N} for _ in range(2)]
        b3S = [{n: A([128, 4, 128], BF16) for n in B3N} for _ in range(2)]
        HB = [{"u": A([128, 4, 128]), "nwT": A([128, 4, 128], BF16), "qd": A([128, 4, 128], BF16), "aT": A([128, 4, 128], BF16),
               "kdec": A([128, 4, 128], BF16), "egl": A([128, 8]), "sz": A([128, 4, 128], BF16)} for _ in range(4)]
        rsm = A([128, 4])
        rf3 = {n: A([128, 4, 128]) for n in ("osb", "osq", "on")}
        rb3 = {n: A([128, 4, 128], BF16) for n in ("vnew", "oab", "oaTt")}
        DMA("sp", Wg, winv[:, :, C_AQ:C_AQ + 1536], reads=WIN, writes=["Wg"])
        DMA("sp", Wz, winv[:, :, C_AZ:C_AZ + 512], reads=WIN, writes=["Wz"])
        with nc.allow_non_contiguous_dma(reason="tiny a/b gate columns"):
            DMA("sp", Wab, winv[:, :, C_AA:C_AA + 8], reads=WIN, writes=["Wab"])
        OP("pool", "memset", [], ["halo"], halo.rearrange("p a b -> p (a b)"), 0.0)
        OP("pool", "memset", [], ["S"], S.rearrange("p a b -> p (a b)"), 0.0)
        OP("pool", "memset", [], ["Sb"], Sb.rearrange("p a b -> p (a b)"), 0.0)
        identf = C("ident")

        def bc_h(ap2d):
            return ap2d.unsqueeze(1).to_broadcast([128, 4, 128])

        def bc_l(ap2d):
            return ap2d.unsqueeze(2).to_broadcast([128, 4, 128])

        def v3(bank):
            return pb[bank][:, :].rearrange("p (h c) -> p h c", h=4)

        def tr4(src_fn, bank_half, dst, dkey, skeys):
            tp = pbf(3)
            o = 512 * bank_half
            for h in range(4):
                OP("pe", "transpose", skeys, ["pb3"], out=tp[:, o + h * 128:o + (h + 1) * 128], in_=src_fn(h), identity=identb[:])
            OP("act", "activation", ["pb3"], [dkey], out=dst, in_=tp[:, o:o + 512].rearrange("p (h t) -> p h t", h=4), func=AF.Copy)

        altb = [0]

        def nb():
            altb[0] ^= 1
            return 4 + altb[0]

        def group_level(G):
            hkeys = ["hT4_%d" % i for i in range(4)]
            for i in range(4):
                t = 4 * G + i
                load_norm_T(xin[t * 128:(t + 1) * 128, :], "attn_g", hT4[:, :, i * 128:(i + 1) * 128], hkeys[i], xtb, junk, xn, ssx, junk_key="sq")
            co, _ = PV["convw"]
            for cc in range(12):
                bk = nb()
                for c in range(8):
                    OP("pe", "matmul", hkeys + ["Wg"], ["pb%d" % bk], pb[bk][:, :], lhsT=Wg[:, c, cc * 128:(cc + 1) * 128], rhs=hT4[:, c, :], start=(c == 0), stop=(c == 7))
                pbu, pk, ck = pbuf[cc % 2], "pbuf%d" % (cc % 2), "cv%d" % cc
                OP("pool", "tensor_copy", ["halo"], [pk], out=pbu[:, 0:4], in_=halo[:, cc, :])
                OP("act", "activation", ["pb%d" % bk], [pk], out=pbu[:, 4:516], in_=pb[bk][:, :], func=AF.Copy)
                wcol = lambda j: pvs[:, co + cc * 4 + j:co + cc * 4 + j + 1]
                OP("dve", "tensor_scalar", [pk, "pvs"], ["rn"], out=rn, in0=pbu[:, 1:513], scalar1=wcol(0), scalar2=None, op0=ALU.mult)
                for j in range(1, 4):
                    OP("dve", "scalar_tensor_tensor", [pk, "rn", "pvs"], ["rn"], out=rn, in0=pbu[:, 1 + j:513 + j], scalar=wcol(j), in1=rn, op0=ALU.mult, op1=ALU.add)
                OP("pool", "tensor_copy", [pk], ["halo"], out=halo[:, cc, :], in_=pbu[:, 512:516])
                OP("act", "activation", ["rn"], [ck], out=cv[:, cc, :], in_=rn, func=AF.Silu)
            for cc in range(8):
                ck = "cv%d" % cc
                OP("pool", "tensor_tensor", [ck], ["sq"], out=sq, in0=cv[:, cc, :], in1=cv[:, cc, :], op=ALU.mult)
                bk2 = nb()
                OP("pe", "matmul", ["sq"], ["pb%d" % bk2], pb[bk2][:, :], lhsT=onesf[:], rhs=sq, start=True, stop=True)
                OP("dve", "tensor_scalar", ["pb%d" % bk2], ["rn"], out=rn, in0=pb[bk2][:, :], scalar1=EPS, scalar2=None, op0=ALU.add)
                OP("act", "activation", ["rn"], ["rn"], out=rn, in_=rn, func=AF.Ln)
                OP("act", "activation", ["rn"], ["rn"], out=rn, in_=rn, func=AF.Exp, scale=-0.5)
                if cc < 4:
                    OP("dve", "scalar_tensor_tensor", [ck, "rn"], ["qTg%d" % cc], out=qTg[:, cc, :], in0=cv[:, cc, :], scalar=128.0 ** -0.5, in1=rn, op0=ALU.mult, op1=ALU.mult)
                else:
                    OP("dve", "tensor_tensor", [ck, "rn"], ["kTg%d" % (cc - 4)], out=kTg[:, cc - 4, :], in0=cv[:, cc, :], in1=rn, op=ALU.mult)
            for h in range(4):
                OP("pool", "tensor_copy", ["cv%d" % (8 + h)], ["vTg%d" % h], out=vTg[:, h, :], in_=cv[:, 8 + h, :])

        hkeys = ["hT4_%d" % i for i in range(4)]
        QK_ = ["qTg%d" % h for h in range(4)]
        KK_ = ["kTg%d" % h for h in range(4)]
        VK_ = ["vTg%d" % h for h in range(4)]

        def prep(G, i):
            t = 4 * G + i
            own = t >= NTC
            ps = i % 2
            hs = i % 4
            cols = slice(i * 128, (i + 1) * 128)
            sm, f3, b3, hb_ = smS[ps], f3S[ps], b3S[ps], HB[hs]
            K_ = lambda n: "%s_%d" % (n, ps)
            H_ = lambda n: "%s_h%d" % (n, hs)
            pair = (4, 5) if ps == 0 else (0, 2)
            tog = [0]

            def nbp():
                tog[0] ^= 1
                return pair[tog[0]]
            so = 64 * ps
            ab = pb[1][:, so:so + 8]
            if own:
                zb = nbp()
                for c in range(8):
                    OP("pe", "matmul", [hkeys[i], "Wz"], ["pb%d" % zb], pb[zb][:, :], lhsT=hT4[:, c, cols], rhs=Wz[:, c, :], start=(c == 0), stop=(c == 7))
            for c in range(8):
                OP("pe", "matmul", [hkeys[i], "Wab"], ["pb1"], ab, lhsT=hT4[:, c, cols], rhs=Wab[:, c, :], start=(c == 0), stop=(c == 7))
            yield
            if own:
                OP("act", "activation", ["pb%d" % zb], [H_("sz")], out=hb_["sz"], in_=v3(zb), func=AF.Silu)
            OP("dve", "tensor_tensor", ["pb1", "pvs"], [K_("xa")], out=sm["xa"], in0=pb[1][:, so:so + 4], in1=PVv("dt_b"), op=ALU.add)
            OP("act", "activation", ["pb1"], [K_("beta")], out=sm["beta"], in_=pb[1][:, so + 4:so + 8], func=AF.Exp, scale=-1.0)
            yield
            OP("act", "activation", [K_("xa")], [K_("xa")], out=sm["xa"], in_=sm["xa"], func=AF.Exp)
            OP("dve", "tensor_scalar", [K_("beta")], [K_("beta")], out=sm["beta"], in0=sm["beta"], scalar1=1.0, scalar2=None, op0=ALU.add)
            yield
            OP("act", "activation", [K_("xa")], [K_("xa")], out=sm["xa"], in_=sm["xa"], func=AF.Ln, bias=oneT[:, 0:1], scale=1.0)
            OP("dve", "reciprocal", [K_("beta")], [K_("beta")], out=sm["beta"], in_=sm["beta"])
            yield
            OP("dve", "tensor_tensor", [K_("xa"), "negA"], [K_("gt")], out=sm["gt"], in0=sm["xa"], in1=negA[:], op=ALU.mult)
            OP("dve", "tensor_scalar", [K_("beta")], [K_("nbeta")], out=sm["nbeta"], in0=sm["beta"], scalar1=-1.0, scalar2=None, op0=ALU.mult)
            yield
            OP("pe", "matmul", [K_("gt")], ["pb1"], pb[1][:, so + 16:so + 20], lhsT=C("tri"), rhs=sm["gt"], start=True, stop=True)
            yield
            OP("dve", "tensor_copy", ["pb1"], [K_("Gc")], out=sm["Gc"], in_=pb[1][:, so + 16:so + 20])
            yield
            OP("pe", "matmul", [K_("Gc")], ["pb1"], pb[1][:, so + 32:so + 36], lhsT=C("sellast"), rhs=sm["Gc"], start=True, stop=True)
            OP("pe", "matmul", [K_("Gc")], ["pb1"], pb[1][:, so + 36:so + 40], lhsT=C("sel63"), rhs=sm["Gc"], start=True, stop=True)
            OP("pe", "matmul", [K_("Gc")], ["pb1"], pb[1][:, so + 40:so + 44], lhsT=C("sel127"), rhs=sm["Gc"], start=True, stop=True)
            OP("act", "activation", [K_("Gc")], [K_("eG")], out=sm["eG"], in_=sm["Gc"], func=AF.Exp)
            OP("dve", "tensor_tensor", [K_("Gc")], [K_("diagG")], out=f3["diagG"], in0=bc_h(identf), in1=bc_l(sm["Gc"]), op=ALU.mult)
            yield
            OP("dve", "tensor_tensor", ["pb1", K_("Gc")], [K_("dgl")], out=sm["dgl"], in0=pb[1][:, so + 32:so + 36], in1=sm["Gc"], op=ALU.subtract)
            OP("act", "activation", ["pb1"], [H_("egl")], out=hb_["egl"], in_=pb[1][:, so + 36:so + 44], func=AF.Exp)
            gsb = nbp()
            OP("pe", "matmul", [K_("diagG")], ["pb%d" % gsb], pb[gsb][:, :], lhsT=onesf[:], rhs=f3["diagG"].rearrange("p h c -> p (h c)"), start=True, stop=True)
            yield
            OP("act", "activation", [K_("dgl")], [K_("ekd")], out=sm["ekd"], in_=sm["dgl"], func=AF.Exp)
            OP("dve", "tensor_tensor", [K_("beta"), K_("eG")], [K_("bg")], out=sm["bg"], in0=sm["beta"], in1=sm["eG"], op=ALU.mult)
            OP("act", "activation", ["pb%d" % gsb], [K_("GSs")], out=f3["GSs"], in_=v3(gsb), func=AF.Copy)
            yield
            OP("act", "activation", [K_("GSs")], [K_("EGb")], out=f3["EGb"], in_=f3["GSs"], func=AF.Exp)
            OP("dve", "tensor_tensor", [K_("GSs"), K_("Gc")], [K_("D")], out=f3["D"], in0=f3["GSs"], in1=bc_l(sm["Gc"]), op=ALU.subtract)
            yield
            OP("dve", "tensor_scalar_max", [K_("D")], [K_("D")], out=f3["D"], in0=f3["D"], scalar1=0.0)
            yield
            OP("act", "activation", [K_("D")], [K_("D")], out=f3["D"], in_=f3["D"], func=AF.Exp, scale=-1.0)
            tr4(lambda h: kTg[:, h, cols], 0, b3["ktm"], K_("ktm"), KK_)
            yield
            OP("pool", "tensor_tensor", [K_("D")], [K_("Dm")], out=f3["Dm"], in0=f3["D"], in1=bc_h(C("mincl")), op=ALU.mult)
            OP("pool", "tensor_tensor", [K_("D")], [K_("Ds")], out=f3["Ds"], in0=f3["D"], in1=bc_h(C("mstrict")), op=ALU.mult)
            tr4(lambda h: vTg[:, h, cols], 1, b3["vtm"], K_("vtm"), VK_)
            yield
            OP("dve", "tensor_tensor", [K_("vtm"), K_("beta")], [K_("vb")], out=b3["vb"], in0=b3["vtm"], in1=bc_l(sm["beta"]), op=ALU.mult)
            OP("pool", "tensor_tensor", [K_("ktm"), K_("bg")], [K_("kbg")], out=b3["kbg"], in0=b3["ktm"], in1=bc_l(sm["bg"]), op=ALU.mult)
            OP("pool", "tensor_tensor", [K_("ktm"), K_("ekd")], [H_("kdec")], out=hb_["kdec"], in0=b3["ktm"], in1=bc_l(sm["ekd"]), op=ALU.mult)
            bk = nbp()
            for h in range(4):
                OP("pe", "matmul", KK_, ["pb%d" % bk], pb[bk][:, h * 128:(h + 1) * 128], lhsT=kTg[:, h, cols], rhs=kTg[:, h, cols], start=True, stop=True)
            yield
            OP("dve", "tensor_tensor", ["pb%d" % bk, K_("Ds")], [K_("L0")], out=f3["L0"], in0=v3(bk), in1=f3["Ds"], op=ALU.mult)
            bk = nbp()
            for h in range(4):
                OP("pe", "matmul", KK_ + QK_, ["pb%d" % bk], pb[bk][:, h * 128:(h + 1) * 128], lhsT=qTg[:, h, cols], rhs=kTg[:, h, cols], start=True, stop=True)
            yield
            OP("dve", "tensor_tensor", [K_("L0"), K_("nbeta")], [K_("X0")], out=b3["X0"], in0=f3["L0"], in1=bc_l(sm["nbeta"]), op=ALU.mult)
            OP("dve", "tensor_tensor", ["pb%d" % bk, K_("Dm")], [K_("Aa")], out=b3["Aa"], in0=v3(bk), in1=f3["Dm"], op=ALU.mult)
            yield
            tr4(lambda h: b3["X0"][:, h, :], 0, b3["Y0"], K_("Y0"), [K_("X0")])
            yield
            tr4(lambda h: b3["Aa"][:, h, :], 1, hb_["aT"], H_("aT"), [K_("Aa")])
            OP("dve", "tensor_tensor", QK_ + [K_("EGb")], [H_("qd")], out=hb_["qd"], in0=qTg[:, :, cols], in1=f3["EGb"], op=ALU.mult)
            yield
            OP("dve", "tensor_tensor", [K_("Y0"), "identb"], [K_("R0")], out=b3["R0"], in0=b3["Y0"], in1=bc_h(identb[:]), op=ALU.add)
            for k in range(5):
                Xk, Yk, Rk = "X%d" % (k % 2), "Y%d" % (k % 2), "R%d" % (k % 2)
                Xn, Yn, Rn = "X%d" % ((k + 1) % 2), "Y%d" % ((k + 1) % 2), "R%d" % ((k + 1) % 2)
                bk = nbp()
                for h in range(4):
                    OP("pe", "matmul", [K_(Xk), K_(Yk)], ["pb%d" % bk], pb[bk][:, h * 128:(h + 1) * 128], lhsT=b3[Yk][:, h, :], rhs=b3[Xk][:, h, :], start=True, stop=True)
                bk2 = nbp()
                if k < 4:
                    for h in range(4):
                        OP("pe", "matmul", [K_(Xk), K_(Yk)], ["pb%d" % bk2], pb[bk2][:, h * 128:(h + 1) * 128], lhsT=b3[Xk][:, h, :], rhs=b3[Yk][:, h, :], start=True, stop=True)
                yield
                OP("act", "activation", ["pb%d" % bk], [K_(Xn)], out=b3[Xn], in_=v3(bk), func=AF.Copy)
                if k < 4:
                    OP("dve", "tensor_copy", ["pb%d" % bk2], [K_(Yn)], out=b3[Yn], in_=v3(bk2))
                yield
                bk = nbp()
                for h in range(4):
                    OP("pe", "matmul", [K_(Xn), K_(Rk)], ["pb%d" % bk], pb[bk][:, h * 128:(h + 1) * 128], lhsT=b3[Xn][:, h, :], rhs=b3[Rk][:, h, :], start=True, stop=True)
                yield
                OP("dve", "tensor_tensor", ["pb%d" % bk, K_(Rk)], [K_(Rn)], out=b3[Rn], in0=v3(bk), in1=b3[Rk], op=ALU.add)
                yield
            R5 = "R1"
            bk = nbp()
            for h in range(4):
                OP("pe", "matmul", [K_(R5), K_("vb")], ["pb%d" % bk], pb[bk][:, h * 128:(h + 1) * 128], lhsT=b3[R5][:, h, :], rhs=b3["vb"][:, h, :], start=True, stop=True)
            bk2 = nbp()
            for h in range(4):
                OP("pe", "matmul", [K_(R5), K_("kbg")], ["pb%d" % bk2], pb[bk2][:, h * 128:(h + 1) * 128], lhsT=b3["kbg"][:, h, :], rhs=b3[R5][:, h, :], start=True, stop=True)
            yield
            OP("act", "activation", ["pb%d" % bk], [H_("u")], out=hb_["u"], in_=v3(bk), func=AF.Copy)
            OP("dve", "tensor_scalar", ["pb%d" % bk2], [H_("nwT")], out=hb_["nwT"], in0=v3(bk2), scalar1=-1.0, scalar2=None, op0=ALU.mult)
            yield

        def rec(G, i):
            t = 4 * G + i
            own = t >= NTC
            hs = i % 4
            hb_ = HB[hs]
            H_ = lambda n: "%s_h%d" % (n, hs)
            for ch in range(2):
                r = slice(64 * ch, 64 * ch + 64)
                bk = 6
                for h in range(4):
                    OP("pe", "matmul", [H_("nwT"), "Sb"], ["pb%d" % bk], pb[bk][r, h * 128:(h + 1) * 128], lhsT=hb_["nwT"][:, h, r], rhs=Sb[:, h, :], start=True, stop=True)
                yield
                OP("dve", "tensor_tensor", ["pb%d" % bk, H_("u")], ["vnew"], out=rb3["vnew"][r], in0=v3(bk)[r], in1=hb_["u"][r], op=ALU.add)
                yield
                for h in range(4):
                    OP("pe", "matmul", [H_("kdec"), "vnew"], ["pb%d" % bk], pb[bk][:, h * 128:(h + 1) * 128], lhsT=hb_["kdec"][r, h, :], rhs=rb3["vnew"][r, h, :], start=True, stop=True)
                if own:
                    for h in range(4):
                        OP("pe", "matmul", [H_("qd"), "Sb"], ["pb7"], pb[7][r, h * 128:(h + 1) * 128], lhsT=hb_["qd"][:, h, r], rhs=Sb[:, h, :], start=True, stop=False)
                        OP("pe", "matmul", [H_("aT"), "vnew"], ["pb7"], pb[7][r, h * 128:(h + 1) * 128], lhsT=hb_["aT"][r, h, r], rhs=rb3["vnew"][r, h, :], start=False, stop=True)
                OP("dve", "tensor_tensor", ["S", H_("egl")], ["S"], out=S, in0=S, in1=bc_l(hb_["egl"][:, 4 * ch:4 * ch + 4]), op=ALU.mult)
                yield
                OP("dve", "tensor_tensor", ["S", "pb%d" % bk], ["S"], out=S, in0=S, in1=v3(bk), op=ALU.add)
                yield
                OP("act", "activation", ["S"], ["Sb"], out=Sb, in_=S, func=AF.Copy)
                yield
            if own:
                OP("act", "activation", ["pb7"], ["osb"], out=rf3["osb"], in_=v3(7), func=AF.Copy)
                yield
                OP("pool", "tensor_tensor", ["osb"], ["osq"], out=rf3["osq"], in0=rf3["osb"], in1=rf3["osb"], op=ALU.mult)
                yield
                OP("dve", "tensor_reduce", ["osq"], ["ssq"], out=rsm, in_=rf3["osq"], axis=AX.X, op=ALU.add)
                yield
                OP("dve", "tensor_scalar", ["ssq"], ["ssq"], out=rsm, in0=rsm, scalar1=1.0 / 128, scalar2=EPS, op0=ALU.mult, op1=ALU.add)
                yield
                OP("act", "activation", ["ssq"], ["ssq"], out=rsm, in_=rsm, func=AF.Ln)
                yield
                OP("act", "activation", ["ssq"], ["ssq"], out=rsm, in_=rsm, func=AF.Exp, scale=-0.5)
                yield
                OP("dve", "tensor_tensor", ["osb", "ssq"], ["on"], out=rf3["on"], in0=rf3["osb"], in1=bc_l(rsm), op=ALU.mult)
                yield
                OP("dve", "tensor_tensor", ["on", "pvs"], ["on"], out=rf3["on"], in0=rf3["on"], in1=bc_h(PVv("gdn_g")), op=ALU.mult)
                yield
                OP("dve", "tensor_tensor", ["on", H_("sz")], ["oab"], out=rb3["oab"], in0=rf3["on"], in1=hb_["sz"], op=ALU.mult)
                yield
                tr4(lambda h: rb3["oab"][:, h, :], 0, rb3["oaTt"], "oaTt", ["oab"])
                tok0 = (t - NTC) * 128
                DMA("sp", oaT_d.rearrange("(h p) t -> p h t", p=128)[:, :, tok0:tok0 + 128], rb3["oaTt"], reads=["oaTt"], writes=["oaT_d"])
                yield

        def chain(*gens):
            for gnr in gens:
                yield from gnr

        def zip_run(*gens):
            gens = list(gens)
            while gens:
                for gnr in list(gens):
                    try:
                        next(gnr)
                    except StopIteration:
                        gens.remove(gnr)

        for G in range(NG):
            group_level(G)
            if G == 0:
                zip_run(prep(G, 0), prep(G, 1))
            else:
                zip_run(prep(G, 0), prep(G, 1), chain(rec(G - 1, 2), rec(G - 1, 3)))
            zip_run(prep(G, 2), prep(G, 3), chain(rec(G, 0), rec(G, 1)))
        zip_run(chain(rec(NG - 1, 2), rec(NG - 1, 3)))

        P.barrier()
        areset()
        xtok = A([128, 4, 1024])
        junk = A([128, 1024], BF16)
        xn = A([128, 1024], BF16)
        ssx = A([128, 1])
        hT4 = A([128, 8, 512], BF16)
        oaT4 = A([128, 4, 512], BF16)
        obT4 = A([128, 4, 512], BF16)
        mT = A([128, 8, 512], BF16)
        upT = A([128, 32, 512], BF16)
        pT = A([128, 2, 512], BF16)
        ptile = [A([128, 256]), A([128, 256])]
        pbf16 = [A([128, 256], BF16), A([128, 256], BF16)]
        sga = [A([128, 512]), A([128, 512])]
        sgb = [A([128, 512]), A([128, 512])]
        m1 = [A([128, 512]), A([128, 512])]
        m2 = [A([128, 512]), A([128, 512])]
        rl_ = [A([128, 512]), A([128, 512])]
        NWB = 6
        wbuf = [A([128, 8, 512], BF16) for _ in range(NWB)]
        wbi = [0]
        WOUT = ["wout_b0", "wout_b1"]
        WPG = ["wpg_b0", "wpg_b1"]
        WUP = ["wup_b%d" % r for r in range(8)]
        WDN = ["wdn_b%d" % r for r in range(8)]

        def wload(src_ap, nk, rkeys):
            i = wbi[0] % NWB
            wbi[0] += 1
            DMA("sp", wbuf[i][:, 0:nk, :], src_ap, reads=rkeys, writes=["wb%d" % i])
            return wbuf[i], "wb%d" % i

        rr = [0]

        def rbank(excl=()):
            while True:
                rr[0] = (rr[0] + 1) % 8
                if rr[0] not in excl:
                    return rr[0]

        def v2(name, shape_rows):
            return name.rearrange("(c p) n -> p c n", p=128)

        wgv = winv
        woav, wobv = v2(woa_b, 512), v2(wob_b, 512)
        woutv, wupv, wdnv, wpgv, wplev = v2(wout_b, 0), v2(wup_b, 0), v2(wdn_b, 0), v2(wpg_b, 0), v2(wple_b, 0)
        for g in range(NQG):
            tok0 = g * 512
            hk = ["hT4_%d" % i for i in range(4)]
            for i in range(4):
                t = NTC + 4 * g + i
                DMA("sp", xtok[:, i, :], xin[t * 128:(t + 1) * 128, :], writes=["xtok%d" % i])
                load_norm_T(None, "attn_g", hT4[:, :, i * 128:(i + 1) * 128], hk[i], None, junk, xn, ssx, keep_x=(xtok[:, i, :], "xtok%d" % i))
            DMA("sp", oaT4, oaT_d.rearrange("(c p) t -> p c t", p=128)[:, :, tok0:tok0 + 512], reads=["oaT_d"], writes=["oaT4"])
            DMA("sp", obT4, obT_d.rearrange("(c p) t -> p c t", p=128)[:, :, tok0:tok0 + 512], reads=["obT_d"], writes=["obT4"])
            for i in range(4):
                pt_, pb_ = ptile[i % 2], pbf16[i % 2]
                DMA("sp", pt_, pin[tok0 + i * 128:tok0 + (i + 1) * 128, :], writes=["ptile%d" % (i % 2)])
                OP("act", "activation", ["ptile%d" % (i % 2)], ["pbf%d" % (i % 2)], out=pb_, in_=pt_, func=AF.Copy)
                tp = pbf(6)
                for c in range(2):
                    OP("pe", "transpose", ["pbf%d" % (i % 2)], ["pb6"], out=tp[:, c * 128:(c + 1) * 128], in_=pb_[:, c * 128:(c + 1) * 128], identity=identb[:])
                OP("act", "activation", ["pb6"], ["pT%d" % i], out=pT[:, :, i * 128:(i + 1) * 128], in_=tp[:, 0:256].rearrange("p (c t) -> p c t", c=2), func=AF.Copy)
            jj = 0
            for cb in range(2):
                wga, kga = wload(wgv[:, :, C_GA + cb * 512:C_GA + (cb + 1) * 512], 8, WIN)
                wgb, kgb = wload(wgv[:, :, C_GB + cb * 512:C_GB + (cb + 1) * 512], 8, WIN)
                woa, koa = wload(woav[:, :, cb * 512:(cb + 1) * 512], 4, ["woa_b"])
                wob, kob = wload(wobv[:, :, cb * 512:(cb + 1) * 512], 4, ["wob_b"])
                for j in range(4):
                    bs = [0, 1, 2, 3] if jj % 2 == 0 else [4, 5, 6, 7]
                    par = jj % 2
                    jj += 1
                    cs_ = slice(j * 128, (j + 1) * 128)
                    for c in range(8):
                        OP("pe", "matmul", hk + [kga], ["pb%d" % bs[0]], pb[bs[0]][:, :], lhsT=wga[:, c, cs_], rhs=hT4[:, c, :], start=(c == 0), stop=(c == 7))
                    for c in range(8):
                        OP("pe", "matmul", hk + [kgb], ["pb%d" % bs[1]], pb[bs[1]][:, :], lhsT=wgb[:, c, cs_], rhs=hT4[:, c, :], start=(c == 0), stop=(c == 7))
                    for c in range(4):
                        OP("pe", "matmul", ["oaT4", koa], ["pb%d" % bs[2]], pb[bs[2]][:, :], lhsT=woa[:, c, cs_], rhs=oaT4[:, c, :], start=(c == 0), stop=(c == 3))
                    for c in range(4):
                        OP("pe", "matmul", ["obT4", kob], ["pb%d" % bs[3]], pb[bs[3]][:, :], lhsT=wob[:, c, cs_], rhs=obT4[:, c, :], start=(c == 0), stop=(c == 3))
                    OP("act", "activation", ["pb%d" % bs[0]], ["sga%d" % par], out=sga[par], in_=pb[bs[0]][:, :], func=AF.Sigmoid)
                    OP("act", "activation", ["pb%d" % bs[1]], ["sgb%d" % par], out=sgb[par], in_=pb[bs[1]][:, :], func=AF.Sigmoid)
                    OP("dve", "tensor_tensor", ["pb%d" % bs[2], "sga%d" % par], ["m1%d" % par], out=m1[par], in0=pb[bs[2]][:, :], in1=sga[par], op=ALU.mult)
                    OP("dve", "tensor_tensor", ["pb%d" % bs[3], "sgb%d" % par], ["m2%d" % par], out=m2[par], in0=pb[bs[3]][:, :], in1=sgb[par], op=ALU.mult)
                    OP("pool", "tensor_tensor", ["m1%d" % par, "m2%d" % par], ["mT%d" % (cb * 4 + j)], out=mT[:, cb * 4 + j, :], in0=m1[par], in1=m2[par], op=ALU.add)
            MT = ["mT%d" % c for c in range(8)]
            for half in range(2):
                hs = slice(half * 512, (half + 1) * 512)
                wo, ko = wload(woutv[:, :, hs], 8, WOUT)
                for i in range(4):
                    bk = rbank()
                    for c in range(8):
                        OP("pe", "matmul", MT + [ko], ["pb%d" % bk], pb[bk][:, :], lhsT=mT[:, c, i * 128:(i + 1) * 128], rhs=wo[:, c, :], start=(c == 0), stop=(c == 7))
                    OP("dve", "tensor_tensor", ["pb%d" % bk, "xtok%d" % i], ["xtok%d" % i], out=xtok[:, i, hs], in0=pb[bk][:, :], in1=xtok[:, i, hs], op=ALU.add)
            for i in range(4):
                load_norm_T(None, "mlp_g", hT4[:, :, i * 128:(i + 1) * 128], hk[i], None, junk, xn, ssx, keep_x=(xtok[:, i, :], "xtok%d" % i))
            ui = 0
            for fb in range(8):
                wu, ku = wload(wupv[:, :, fb * 512:(fb + 1) * 512], 8, WUP)
                for j in range(4):
                    bk = rbank()
                    for c in range(8):
                        OP("pe", "matmul", hk + [ku], ["pb%d" % bk], pb[bk][:, :], lhsT=wu[:, c, j * 128:(j + 1) * 128], rhs=hT4[:, c, :], start=(c == 0), stop=(c == 7))
                    rb = rl_[ui % 2]
                    rk = "rl%d" % (ui % 2)
                    ui += 1
                    OP("act", "activation", ["pb%d" % bk], [rk], out=rb, in_=pb[bk][:, :], func=AF.Relu)
                    OP("pool", "tensor_tensor", [rk], ["upT%d" % (fb * 4 + j)], out=upT[:, fb * 4 + j, :], in0=rb, in1=rb, op=ALU.mult)
            for half in range(2):
                hs = slice(half * 512, (half + 1) * 512)
                bs = [0, 1, 2, 3] if half == 0 else [4, 5, 6, 7]
                for fq in range(4):
                    wd, kd = wload(wdnv[:, fq * 8:(fq + 1) * 8, hs], 8, WDN)
                    for i in range(4):
                        for c in range(8):
                            f = fq * 8 + c
                            OP("pe", "matmul", ["upT%d" % f, kd], ["pb%d" % bs[i]], pb[bs[i]][:, :], lhsT=upT[:, f, i * 128:(i + 1) * 128], rhs=wd[:, c, :], start=(f == 0), stop=(f == 31))
                for i in range(4):
                    OP("dve", "tensor_tensor", ["pb%d" % bs[i], "xtok%d" % i], ["xtok%d" % i], out=xtok[:, i, hs], in0=pb[bs[i]][:, :], in1=xtok[:, i, hs], op=ALU.add)
            for i in range(4):
                load_norm_T(None, "ple_g", hT4[:, :, i * 128:(i + 1) * 128], hk[i], None, junk, xn, ssx, keep_x=(xtok[:, i, :], "xtok%d" % i))
            gi = 0
            for half in range(2):
                hs = slice(half * 512, (half + 1) * 512)
                wp, kp = wload(wpgv[:, :, hs], 8, WPG)
                wl, kl = wload(wplev[:, :, hs], 2, ["wple_b"])
                for i in range(4):
                    b1 = rbank()
                    b2 = rbank()
                    for c in range(8):
                        OP("pe", "matmul", [hk[i], kp], ["pb%d" % b1], pb[b1][:, :], lhsT=hT4[:, c, i * 128:(i + 1) * 128], rhs=wp[:, c, :], start=(c == 0), stop=(c == 7))
                    for c in range(2):
                        OP("pe", "matmul", ["pT%d" % i, kl], ["pb%d" % b2], pb[b2][:, :], lhsT=pT[:, c, i * 128:(i + 1) * 128], rhs=wl[:, c, :], start=(c == 0), stop=(c == 1))
                    par = gi % 2
                    gi += 1
                    OP("act", "activation", ["pb%d" % b1], ["sga%d" % par], out=sga[par], in_=pb[b1][:, :], func=AF.Sigmoid)
                    OP("dve", "tensor_tensor", ["pb%d" % b2, "sga%d" % par], ["m1%d" % par], out=m1[par], in0=pb[b2][:, :], in1=sga[par], op=ALU.mult)
                    OP("pool", "tensor_tensor", ["m1%d" % par, "xtok%d" % i], ["xtok%d" % i], out=xtok[:, i, hs], in0=m1[par], in1=xtok[:, i, hs], op=ALU.add)
            for i in range(4):
                DMA("sp", out_d[tok0 + i * 128:tok0 + (i + 1) * 128, :], xtok[:, i, :], reads=["xtok%d" % i], writes=["out%d_%d" % (g, i)])

        if debug:
            P.barrier()
            DMA("pool", dbg["obT"][:, :], obT_d[:, :])
            DMA("pool", dbg["oaT"][:, :], oaT_d[:, :])
        P.final_wait("sp")
        P.emit(nc, ctx)
    return nc


N_CORES = 8
B, T, D = 4, 8192, 1024


def make_in_maps(inp, TC, TO, pairs):
    cst = host_consts()
    pv = host_pvec(inp)
    maps = []
    x = np.asarray(inp["x"], np.float32)
    p = np.asarray(inp["p"], np.float32)[0]
    pos = np.asarray(inp["positions"], np.int32)
    shared = {
        "cst": cst, "pv": pv, "cmask": host_cmask(),
        "w_in": np.ascontiguousarray(inp["w_in"][0], np.float32),
        "w_o_a": np.ascontiguousarray(inp["w_o_a"][0], np.float32),
        "w_o_b": np.ascontiguousarray(inp["w_o_b"][0], np.float32),
        "w_out": np.ascontiguousarray(inp["w_out"][0], np.float32),
        "w_up": np.ascontiguousarray(inp["w_up"][0], np.float32),
        "w_down": np.ascontiguousarray(inp["w_down"][0], np.float32),
        "w_pg": np.ascontiguousarray(inp["w_ple_gate"][0], np.float32),
        "w_ple": np.ascontiguousarray(inp["w_ple"][0], np.float32),
    }
    for (b, s) in pairs:
        xin = np.zeros((TC + TO, D), np.float32)
        posl = np.zeros((TC + TO,), np.int32)
        if s == 0:
            xin[TC:] = x[b, 0:TO]
            posl[TC:] = pos[b, 0:TO]
            ctxm = np.full((128, 1), -30000.0, np.float32)
            pin = p[b, 0:TO]
        else:
            xin[:] = x[b, 0:TC + TO]
            posl[:] = pos[b, 0:TC + TO]
            ctxm = np.zeros((128, 1), np.float32)
            pin = p[b, TC:TC + TO]
        m = dict(shared)
        m["xin"] = xin
        m["pin"] = np.ascontiguousarray(pin)
        m["pos"] = np.ascontiguousarray(posl.reshape(-1, 128).T)
        m["ctxm"] = ctxm
        maps.append(m)
    return maps


def kernel(**inp):
    TC = TO = T // 2
    nc = build(TC, TO)
    pairs = [(b, s) for b in range(B) for s in range(2)]
    maps = make_in_maps(inp, TC, TO, pairs)
    res = run_bass_kernel_spmd(nc, maps, core_ids=list(range(N_CORES)))
    out = np.zeros((B, T, D), np.float32)
    for i, (b, s) in enumerate(pairs):
        out[b, s * TO:(s + 1) * TO] = res.results[i]["out"]
    return out
```

```python
import contextlib
import math
import numpy as np
import concourse.bass as bass
import concourse.mybir as mybir
from concourse.bass_utils import run_bass_kernel_spmd

F32 = mybir.dt.float32
BF16 = mybir.dt.bfloat16
I32 = mybir.dt.int32
U8 = mybir.dt.uint8
AF = mybir.ActivationFunctionType
ALU = mybir.AluOpType
AX = mybir.AxisListType

NDS = 28
NDS_SP = 20
EPS = 1e-6
PI = math.pi


class Prog:
    ENGS = ("pe", "act", "dve", "pool", "sp")

    def __init__(self):
        self.streams = {e: [] for e in self.ENGS}
        self.cnt = {e: 0 for e in self.ENGS}
        self.waited = {e: {} for e in self.ENGS}
        self.lastw = {}
        self.readers = {}
        self.ndma = 0
        self.nq = {}
        self.slotval = {}

    def _deps(self, reads, writes):
        toks = []
        for k in reads:
            t = self.lastw.get(k)
            if t is not None:
                toks.append(t)
        for k in writes:
            t = self.lastw.get(k)
            if t is not None:
                toks.append(t)
            toks.extend(self.readers.get(k, ()))
        return toks

    def _commit(self, tok, reads, writes):
        for k in reads:
            self.readers.setdefault(k, []).append(tok)
        for k in writes:
            self.lastw[k] = tok
            self.readers[k] = []

    def _need(self, eng, toks):
        best = {}
        for sk, val in toks:
            if sk == "pe" and eng == "pe":
                continue
            if val > best.get(sk, 0):
                best[sk] = val
        out = []
        w = self.waited[eng]
        for sk, val in best.items():
            if w.get(sk, 0) >= val:
                continue
            w[sk] = val
            out.append((sk, val))
        return out

    def op(self, eng, fn, reads=(), writes=()):
        pr = [k for k in reads if k.startswith("pb")]
        if pr:
            reads = [k for k in reads if not k.startswith("pb")]
            writes = list(writes) + pr
        toks = self._deps(reads, writes)
        waits = self._need(eng, toks)
        self.cnt[eng] += 1
        tok = (eng, self.cnt[eng])
        self.streams[eng].append((waits, fn, None))
        self._commit(tok, reads, writes)
        return tok

    def dma(self, q, fn, reads=(), writes=()):
        lo, n = (0, NDS_SP) if q == "sp" else (NDS_SP, NDS - NDS_SP)
        k = self.nq.get(q, 0)
        self.nq[q] = k + 1
        slot = lo + k % n
        val = 16 * (k // n + 1)
        self.ndma += 1
        self.slotval[slot] = val
        toks = self._deps(reads, writes)
        if val > 16:
            toks.append((("d", slot), val - 16))
        waits = self._need(q, toks)
        tok = (("d", slot), val)
        self.streams[q].append((waits, fn, slot))
        self._commit(tok, reads, writes)
        return tok

    def _all_toks(self):
        toks = [(e, self.cnt[e]) for e in self.ENGS if self.cnt[e] > 0]
        for s, v in self.slotval.items():
            toks.append((("d", s), v))
        return toks

    def barrier(self):
        toks = self._all_toks()
        for e in self.ENGS:
            waits = self._need(e, toks)
            if waits:
                self.streams[e].append((waits, None, None))
        self.lastw = {}
        self.readers = {}

    def sync_engines(self):
        toks = [(e, self.cnt[e]) for e in self.ENGS if self.cnt[e] > 0]
        for e in self.ENGS:
            waits = self._need(e, toks)
            if waits:
                self.streams[e].append((waits, None, None))
        self.lastw = {k: t for k, t in self.lastw.items() if not isinstance(t[0], str)}
        self.readers = {k: [t for t in v if not isinstance(t[0], str)] for k, v in self.readers.items()}

    def final_wait(self, eng="sp"):
        waits = self._need(eng, self._all_toks())
        self.streams[eng].append((waits, None, None))

    def emit(self, nc, ctx):
        esem = {e: ctx.enter_context(nc.semaphore("s_" + e)) for e in self.ENGS}
        dsem = [ctx.enter_context(nc.semaphore("d_%d" % i)) for i in range(NDS)]

        def semof(sk):
            return esem[sk] if isinstance(sk, str) else dsem[sk[1]]

        block = ctx.enter_context(nc.Block())

        def run(e, eng):
            for waits, fn, slot in self.streams[e]:
                for sk, val in waits:
                    eng.wait_ge(semof(sk), val)
                if fn is None:
                    continue
                ins = fn(eng)
                if slot is None:
                    ins.then_inc(esem[e], 1)
                else:
                    ins.then_inc(dsem[slot], 16)

        @block.tensor
        def _(eng):
            run("pe", eng)

        @block.scalar
        def _(eng):
            run("act", eng)

        @block.vector
        def _(eng):
            run("dve", eng)

        @block.gpsimd
        def _(eng):
            run("pool", eng)

        @block.sync
        def _(eng):
            run("sp", eng)


def _layout(items):
    off = {}
    o = 0
    for name, w in items:
        off[name] = (o, w)
        o += w
    return off, o


CST_ITEMS = [("ident", 128), ("tri", 128), ("mincl", 128), ("mstrict", 128), ("sellast", 128),
             ("sel63", 128), ("sel127", 128), ("invf", 8)]
CST, NCST = _layout(CST_ITEMS)
PV_ITEMS = [("attn_g", 8), ("mlp_g", 8), ("ple_g", 8), ("convw", 48), ("gdn_g", 128), ("q_g", 64), ("k_g", 64),
            ("lq1", 64), ("lk1", 64), ("lq2", 64), ("lk2", 64), ("diff_g", 1), ("a_log", 4), ("dt_b", 4)]
PV, NPV = _layout(PV_ITEMS)

C_AQ, C_AK, C_AV, C_AZ, C_AA, C_AB, C_BQ, C_BK, C_BV, C_GA, C_GB, C_END = (
    0, 512, 1024, 1536, 2048, 2052, 2056, 2568, 3080, 3592, 4616, 5640)


def host_consts():
    c = np.zeros((128, NCST), np.float32)
    idx = np.arange(128)
    same = (idx[:, None] // 64) == (idx[None, :] // 64)

    def put(name, a):
        o, w = CST[name]
        c[:, o:o + w] = a.reshape(128, w)

    put("ident", np.eye(128, dtype=np.float32))
    put("tri", (same & (idx[:, None] <= idx[None, :])).astype(np.float32))
    put("mincl", (same & (idx[:, None] >= idx[None, :])).astype(np.float32))
    put("mstrict", (same & (idx[:, None] > idx[None, :])).astype(np.float32))
    put("sellast", (idx[:, None] == (64 * (idx[None, :] // 64) + 63)).astype(np.float32))
    put("sel63", np.broadcast_to((idx[:, None] == 63), (128, 128)).astype(np.float32))
    put("sel127", np.broadcast_to((idx[:, None] == 127), (128, 128)).astype(np.float32))
    rot = 16
    invf = (500000.0 ** (-np.arange(0, rot, 2, dtype=np.float32) / rot)).astype(np.float32)
    put("invf", np.broadcast_to(invf[None, :], (128, 8)))
    return c


def host_cmask():
    idx = np.arange(128)
    q = np.arange(512)
    cm = np.stack([((128 * m + idx[:, None]) <= q[None, :]).astype(np.float32) for m in range(4)], 1)
    return np.ascontiguousarray(cm.reshape(128, 2048))


def host_pvec(inp):
    v = np.zeros((128, NPV), np.float32)

    def put(name, a):
        o, w = PV[name]
        v[:, o:o + w] = np.asarray(a, np.float32).reshape(128, w)

    def pc(g):
        return np.ascontiguousarray(np.asarray(g, np.float32).reshape(8, 128).T)

    def bc(g):
        g = np.asarray(g, np.float32).reshape(1, -1)
        return np.broadcast_to(g, (128, g.shape[1]))

    put("attn_g", pc(inp["attn_norm"][0]))
    put("mlp_g", pc(inp["mlp_norm"][0]))
    put("ple_g", pc(inp["ple_norm"][0]))
    cw = np.asarray(inp["conv_w"][0], np.float32)
    put("convw", np.ascontiguousarray(cw.reshape(4, 12, 128).transpose(2, 1, 0)))
    put("gdn_g", bc(inp["gdn_norm"][0]))
    put("q_g", bc(inp["q_norm"][0]))
    put("k_g", bc(inp["k_norm"][0]))
    put("lq1", bc(inp["lambda_q1"][0]))
    put("lk1", bc(inp["lambda_k1"][0]))
    put("lq2", bc(inp["lambda_q2"][0]))
    put("lk2", bc(inp["lambda_k2"][0]))
    put("diff_g", np.asarray(inp["diff_norm"][0], np.float32).reshape(128, 1))
    put("a_log", bc(inp["a_log"][0]))
    put("dt_b", bc(inp["dt_bias"][0]))
    return v


def build(TC, TO, debug=False, stage=99):
    NTOK = TC + TO
    NT = NTOK // 128
    NTC = TC // 128
    NQG = TO // 512
    NG = NTOK // 512
    nc = bass.Bass("TRN2", target_bir_lowering=False)

    def din(name, shape, dt=F32):
        return nc.dram_tensor(name, shape, dt, kind="ExternalInput").ap()

    xin = din("xin", [NTOK, 1024])
    pin = din("pin", [TO, 256])
    pos_d = din("pos", [128, NT], I32)
    ctxm_d = din("ctxm", [128, 1])
    cst_d = din("cst", [128, NCST])
    cmask_d = din("cmask", [128, 2048])
    pv_d = din("pv", [128, NPV])
    w_in = din("w_in", [1024, 5640])
    w_o_a = din("w_o_a", [512, 1024])
    w_o_b = din("w_o_b", [512, 1024])
    w_out = din("w_out", [1024, 1024])
    w_up = din("w_up", [1024, 4096])
    w_down = din("w_down", [4096, 1024])
    w_pg = din("w_pg", [1024, 1024])
    w_ple = din("w_ple", [256, 1024])
    out_d = nc.dram_tensor("out", [TO, 1024], F32, kind="ExternalOutput").ap()

    def dscr(name, shape, dt=BF16):
        return nc.dram_tensor(name, shape, dt).ap()

    win_b = dscr("win_b", [1024, 5640])
    woa_b = dscr("woa_b", [512, 1024])
    wob_b = dscr("wob_b", [512, 1024])
    wout_b = dscr("wout_b", [1024, 1024])
    wup_b = dscr("wup_b", [1024, 4096])
    wdn_b = dscr("wdn_b", [4096, 1024])
    wpg_b = dscr("wpg_b", [1024, 1024])
    wple_b = dscr("wple_b", [256, 1024])
    obT_d = dscr("obT_d", [512, TO])
    oaT_d = dscr("oaT_d", [512, TO])
    dbg = {}
    if debug:
        dbg["obT"] = nc.dram_tensor("dbg_obT", [512, TO], F32, kind="ExternalOutput").ap()
        dbg["oaT"] = nc.dram_tensor("dbg_oaT", [512, TO], F32, kind="ExternalOutput").ap()

    P = Prog()
    ctx = contextlib.ExitStack()
    with ctx:
        def sb(name, shape, dt=F32):
            return ctx.enter_context(nc.sbuf_tensor(name, shape, dt))

        cs = sb("cs", [128, NCST])
        pvs = sb("pvs", [128, NPV])

        def C(name):
            o, w = CST[name]
            return cs[:, o:o + w]

        def PVv(name):
            o, w = PV[name]
            return pvs[:, o:o + w]

        identb = sb("identb", [128, 128], BF16)
        onesb = sb("onesb", [128, 128], BF16)
        onesf = sb("onesf", [128, 128])
        cmaskb = sb("cmaskb", [128, 4, 512], BF16)
        epsT = sb("epsT", [128, 1])
        oneT = sb("oneT", [128, 1])
        nhalf = sb("nhalf", [128, 1])
        phalf = sb("phalf", [128, 1])
        posi = sb("posi", [128, NT], I32)
        posf = sb("posf", [128, NT])
        sinT = sb("sinT", [128, NT, 8])
        cosT = sb("cosT", [128, NT, 8])
        ctxm = sb("ctxm_s", [128, 1])
        negshift = sb("negshift", [128, 1])
        biasC = sb("biasC", [128, 1])
        neglam = sb("neglam", [128, 1])
        gscale = sb("gscale", [128, 1])
        negA = sb("negA", [128, 4])
        tmp64 = sb("tmp64", [128, 64])
        tmp1 = sb("tmp1", [128, 4])
        pb = [ctx.enter_context(nc.psum_tensor("pb%d" % i, [128, 512], F32)) for i in range(8)]

        def pbf(i):
            return pb[i][:, :].bitcast(BF16)

        ARENA = 188 * 1024
        arena = sb("arena", [128, ARENA], U8)
        apos = [0]

        def areset():
            apos[0] = 0

        def A(shape, dt=F32):
            esz = {F32: 4, BF16: 2, I32: 4}[dt]
            n = int(np.prod(shape[1:])) * esz
            o = apos[0]
            apos[0] = (o + n + 63) // 64 * 64
            assert apos[0] <= ARENA, ("arena overflow", apos[0])
            v = arena[:, o:o + n].bitcast(dt)
            if len(shape) == 3:
                v = v.rearrange("p (a b) -> p a b", a=shape[1])
            elif len(shape) == 4:
                v = v.rearrange("p (a b c) -> p a b c", a=shape[1], b=shape[2])
            return v

        def OP(eng, name, reads, writes, *args, **kw):
            P.op(eng, lambda e: getattr(e, name)(*args, **kw), reads, writes)

        def DMA(q, out, in_, reads=(), writes=()):
            P.dma(q, lambda e: e.dma_start(out=out, in_=in_), reads, writes)

        apos[0] = 180 * 1024
        angT = A([128, NT, 8])
        kfT = A([128, NT, 8])
        kiT = A([128, NT, 8], I32)
        DMA("sp", cs[:], cst_d[:, :], writes=["cs"])
        DMA("sp", pvs[:], pv_d[:, :], writes=["pvs"])
        DMA("sp", posi[:], pos_d[:, :], writes=["posi"])
        DMA("sp", ctxm[:], ctxm_d[:, :], writes=["ctxm"])
        for r in range(2):
            DMA("pool", win_b[r * 512:(r + 1) * 512, C_BK:C_BV + 512], w_in[r * 512:(r + 1) * 512, C_BK:C_BV + 512], writes=["win_kv"])
        for r in range(8):
            DMA("pool", win_b[r * 128:(r + 1) * 128, 0:C_BK], w_in[r * 128:(r + 1) * 128, 0:C_BK], writes=["win_b%d" % r])
        for r in range(8):
            DMA("pool", win_b[r * 128:(r + 1) * 128, C_GA:C_END], w_in[r * 128:(r + 1) * 128, C_GA:C_END], writes=["win_c%d" % r])
        WIN = ["win_b%d" % r for r in range(8)] + ["win_c%d" % r for r in range(8)]
        DMA("pool", woa_b[:, :], w_o_a[:, :], writes=["woa_b"])
        DMA("pool", wob_b[:, :], w_o_b[:, :], writes=["wob_b"])
        for r in range(2):
            DMA("pool", wout_b[r * 512:(r + 1) * 512, :], w_out[r * 512:(r + 1) * 512, :], writes=["wout_b%d" % r])
            DMA("pool", wpg_b[r * 512:(r + 1) * 512, :], w_pg[r * 512:(r + 1) * 512, :], writes=["wpg_b%d" % r])
        for r in range(8):
            DMA("pool", wup_b[r * 128:(r + 1) * 128, :], w_up[r * 128:(r + 1) * 128, :], writes=["wup_b%d" % r])
        for r in range(8):
            DMA("pool", wdn_b[r * 512:(r + 1) * 512, :], w_down[r * 512:(r + 1) * 512, :], writes=["wdn_b%d" % r])
        DMA("pool", wple_b[:, :], w_ple[:, :], writes=["wple_b"])

        OP("dve", "tensor_copy", ["cs"], ["identb"], out=identb[:], in_=C("ident"))
        OP("pool", "memset", [], ["onesb"], onesb[:], 1.0)
        OP("pool", "memset", [], ["onesf"], onesf[:], 1.0)
        OP("pool", "memset", [], ["epsT"], epsT[:], EPS)
        OP("pool", "memset", [], ["oneT"], oneT[:], 1.0)
        OP("pool", "memset", [], ["nhalf"], nhalf[:], -0.5)
        OP("pool", "memset", [], ["phalf"], phalf[:], 0.5)
        DMA("pool", cmaskb[:].rearrange("p a b -> p (a b)"), cmask_d[:, :], writes=["cmaskb"])
        OP("dve", "tensor_copy", ["posi"], ["posf"], out=posf[:], in_=posi[:])
        OP("dve", "tensor_tensor", ["posf", "cs"], ["angT"], out=angT[:],
           in0=posf[:, :].unsqueeze(2).to_broadcast([128, NT, 8]),
           in1=C("invf").unsqueeze(1).to_broadcast([128, NT, 8]), op=ALU.mult)

        def sin_table(dst, shift, key):
            if shift != 0.0:
                OP("dve", "tensor_scalar", ["angT"], ["angS"], out=dst[:], in0=angT[:], scalar1=shift, scalar2=None, op0=ALU.add)
                src, skey = dst, "angS"
            else:
                src, skey = angT, "angT"
            OP("dve", "tensor_scalar", [skey], ["kfT"], out=kfT[:], in0=src[:], scalar1=1.0 / (2 * PI), scalar2=None, op0=ALU.mult)
            OP("dve", "tensor_copy", ["kfT"], ["kiT"], out=kiT[:], in_=kfT[:])
            OP("dve", "tensor_copy", ["kiT"], ["kfT"], out=kfT[:], in_=kiT[:])
            OP("dve", "scalar_tensor_tensor", ["kfT", skey], ["angR"], out=kfT[:], in0=kfT[:], scalar=-2 * PI, in1=src[:], op0=ALU.mult, op1=ALU.add)
            OP("dve", "tensor_scalar", ["angR"], ["angR"], out=kfT[:], in0=kfT[:], scalar1=3.1415925, scalar2=-3.1415925, op0=ALU.min, op1=ALU.max)
            OP("act", "activation", ["angR"], [key, "angS", "kfT"], out=dst[:], in_=kfT[:], func=AF.Sin)

        sin_table(cosT, PI / 2, "cosT")
        sin_table(sinT, 0.0, "sinT")
        OP("dve", "tensor_tensor", ["pvs"], ["tmp64"], out=tmp64[:], in0=PVv("q_g"), in1=PVv("q_g"), op=ALU.mult)
        OP("dve", "tensor_reduce", ["tmp64"], ["tmp1a"], out=tmp1[:, 0:1], in_=tmp64[:], axis=AX.X, op=ALU.max)
        OP("dve", "tensor_tensor", ["pvs", "tmp1a"], ["tmp64"], out=tmp64[:], in0=PVv("k_g"), in1=PVv("k_g"), op=ALU.mult)
        OP("dve", "tensor_reduce", ["tmp64"], ["tmp1b"], out=tmp1[:, 1:2], in_=tmp64[:], axis=AX.X, op=ALU.max)
        OP("dve", "tensor_tensor", ["tmp1a", "tmp1b"], ["negshift"], out=negshift[:], in0=tmp1[:, 0:1], in1=tmp1[:, 1:2], op=ALU.mult)
        OP("pool", "tensor_tensor", ["negshift", "phalf"], ["negshift"], out=negshift[:], in0=negshift[:], in1=phalf[:], op=ALU.pow)
        OP("dve", "tensor_scalar", ["negshift"], ["negshift"], out=negshift[:], in0=negshift[:], scalar1=-8.0, scalar2=None, op0=ALU.mult)
        OP("dve", "tensor_tensor", ["negshift", "ctxm"], ["biasC"], out=biasC[:], in0=negshift[:], in1=ctxm[:], op=ALU.add)
        OP("dve", "tensor_tensor", ["pvs", "tmp1b"], ["tmp64"], out=tmp64[:], in0=PVv("lq1"), in1=PVv("lk1"), op=ALU.mult)
        OP("dve", "tensor_reduce", ["tmp64"], ["tmp1c"], out=tmp1[:, 2:3], in_=tmp64[:], axis=AX.X, op=ALU.add)
        OP("dve", "tensor_tensor", ["pvs", "tmp1c"], ["tmp64"], out=tmp64[:], in0=PVv("lq2"), in1=PVv("lk2"), op=ALU.mult)
        OP("dve", "tensor_reduce", ["tmp64"], ["tmp1d"], out=tmp1[:, 3:4], in_=tmp64[:], axis=AX.X, op=ALU.add)
        OP("act", "activation", ["tmp1c", "tmp1d"], ["tmp1e"], out=tmp1[:, 2:4], in_=tmp1[:, 2:4], func=AF.Exp)
        OP("dve", "tensor_tensor", ["tmp1e"], ["neglam"], out=neglam[:], in0=tmp1[:, 3:4], in1=tmp1[:, 2:3], op=ALU.subtract)
        lam_init = 0.8 - 0.6 * math.exp(-0.3 * 0)
        OP("dve", "tensor_scalar", ["neglam"], ["neglam"], out=neglam[:], in0=neglam[:], scalar1=-lam_init, scalar2=None, op0=ALU.add)
        OP("dve", "tensor_scalar", ["pvs"], ["gscale"], out=gscale[:], in0=PVv("diff_g"), scalar1=1.0 - lam_init, scalar2=None, op0=ALU.mult)
        OP("act", "activation", ["pvs"], ["negA"], out=negA[:], in_=PVv("a_log"), func=AF.Exp)
        OP("dve", "tensor_scalar", ["negA"], ["negA"], out=negA[:], in0=negA[:], scalar1=-1.0, scalar2=None, op0=ALU.mult)

        P.sync_engines()

        state = {"xi": 0}

        def load_norm_T(x_rows, gname, hT_dst, hkey, xt_bufs, junk, xn, ss, keep_x=None, junk_key="junk"):
            if keep_x is None:
                i = state["xi"] % len(xt_bufs)
                state["xi"] += 1
                xt, xk = xt_bufs[i], "xt%d" % i
                DMA("sp", xt, x_rows, writes=[xk])
            else:
                xt, xk = keep_x
            OP("act", "activation", [xk], [junk_key, "ss"], out=junk, in_=xt, func=AF.Square, accum_out=ss[:, 0:1])
            OP("dve", "tensor_scalar", ["ss"], ["ss"], out=ss[:, 0:1], in0=ss[:, 0:1], scalar1=1.0 / 1024, scalar2=EPS, op0=ALU.mult, op1=ALU.add)
            OP("act", "activation", ["ss"], ["ss"], out=ss[:, 0:1], in_=ss[:, 0:1], func=AF.Ln)
            OP("act", "activation", ["ss"], ["ss"], out=ss[:, 0:1], in_=ss[:, 0:1], func=AF.Exp, scale=-0.5)
            OP("dve", "tensor_scalar", [xk, "ss"], ["xn"], out=xn, in0=xt, scalar1=ss[:, 0:1], scalar2=None, op0=ALU.mult)
            tp = pbf(6)
            for c in range(8):
                OP("pe", "transpose", ["xn"], ["pb6"], out=tp[:, c * 128:(c + 1) * 128], in_=xn[:, c * 128:(c + 1) * 128], identity=identb[:])
            OP("dve", "tensor_tensor", ["pb6", "pvs"], [hkey], out=hT_dst,
               in0=tp.rearrange("p (c t) -> p c t", c=8),
               in1=PVv(gname).unsqueeze(2).to_broadcast([128, 8, 128]), op=ALU.mult)

        def qk_post(bank, gname, t, dest, dkey, W):
            kf, ksq, ssk, kbb, rt = W["kf"], W["ksq"], W["ssk"], W["kbb"], W["rt"]
            bk = "pb%d" % bank
            OP("act", "activation", [bk], ["kf"], out=kf, in_=pb[bank][:, :], func=AF.Copy)
            OP("pool", "tensor_tensor", ["kf"], ["ksq"], out=ksq, in0=kf, in1=kf, op=ALU.mult)
            OP("dve", "tensor_reduce", ["ksq"], ["ssk"], out=ssk, in_=ksq.rearrange("p (g d) -> p g d", g=8), axis=AX.X, op=ALU.add)
            OP("dve", "tensor_scalar", ["ssk"], ["ssk"], out=ssk, in0=ssk, scalar1=1.0 / 64, scalar2=EPS, op0=ALU.mult, op1=ALU.add)
            OP("act", "activation", ["ssk"], ["ssk"], out=ssk, in_=ssk, func=AF.Ln)
            OP("act", "activation", ["ssk"], ["ssk"], out=ssk, in_=ssk, func=AF.Exp, scale=-0.5)
            kf3 = kf.rearrange("p (g d) -> p g d", g=8)
            OP("dve", "tensor_tensor", ["kf", "ssk"], ["kf"], out=kf3, in0=kf3, in1=ssk.unsqueeze(2).to_broadcast([128, 8, 64]), op=ALU.mult)
            OP("dve", "tensor_tensor", ["kf", "pvs"], ["kf"], out=kf3, in0=kf3, in1=PVv(gname).unsqueeze(1).to_broadcast([128, 8, 64]), op=ALU.mult)
            kb3 = kbb.rearrange("p (g d) -> p g d", g=8)
            OP("act", "activation", ["kf"], ["kbb"], out=kbb, in_=kf, func=AF.Copy)
            cb = cosT[:, t, :].unsqueeze(1).to_broadcast([128, 8, 8])
            sn = sinT[:, t, :].unsqueeze(1).to_broadcast([128, 8, 8])
            x1, x2 = kf3[:, :, 0:8], kf3[:, :, 8:16]
            OP("dve", "tensor_tensor", ["kf", "cosT"], ["rt0"], out=rt[:, 0], in0=x1, in1=cb, op=ALU.mult)
            OP("dve", "tensor_tensor", ["kf", "sinT"], ["rt1"], out=rt[:, 1], in0=x2, in1=sn, op=ALU.mult)
            OP("dve", "tensor_tensor", ["kf", "cosT"], ["rt2"], out=rt[:, 2], in0=x2, in1=cb, op=ALU.mult)
            OP("dve", "tensor_tensor", ["kf", "sinT"], ["rt3"], out=rt[:, 3], in0=x1, in1=sn, op=ALU.mult)
            OP("dve", "tensor_tensor", ["rt0", "rt1"], ["kbb"], out=kb3[:, :, 0:8], in0=rt[:, 0], in1=rt[:, 1], op=ALU.subtract)
            OP("dve", "tensor_tensor", ["rt2", "rt3"], ["kbb"], out=kb3[:, :, 8:16], in0=rt[:, 2], in1=rt[:, 3], op=ALU.add)
            tp = pbf(6)
            for h in range(4):
                OP("pe", "transpose", ["kbb"], ["pb6"], out=tp[:, h * 128:(h + 1) * 128], in_=kbb[:, h * 128:(h + 1) * 128], identity=identb[:])
            OP("act", "activation", ["pb6"], [dkey], out=dest, in_=tp[:, 0:512].rearrange("p (h t) -> p h t", h=4), func=AF.Copy)

        areset()
        KT = A([128, 4, NTOK], BF16)
        Vt = A([128, NT, 512], BF16)

        def p1set():
            d = {"xt": A([128, 1024]), "xn": A([128, 1024], BF16), "hT": A([128, 8, 128], BF16),
                 "ss": A([128, 1]), "kf": A([128, 512]), "ksq": A([128, 512]), "ssk": A([128, 8]), "kbb": A([128, 512], BF16),
                 "rt": A([128, 4, 8, 8])}
            d["junk"] = d["ksq"].bitcast(BF16)
            return d
        set0 = p1set()
        xtb = [set0["xt"]]
        junk, xn, hT1, ssx = set0["junk"], set0["xn"], set0["hT"], set0["ss"]
        Wqk = set0
        mark_kv = apos[0]
        Wkv = A([128, 8, 1024], BF16)
        set1 = p1set()
        set2 = p1set()
        winv = win_b.rearrange("(c p) n -> p c n", p=128)
        DMA("sp", Wkv[:, :, 0:512], winv[:, :, C_BK:C_BK + 512], reads=["win_kv"], writes=["Wkv"])
        DMA("sp", Wkv[:, :, 512:1024], winv[:, :, C_BV:C_BV + 512], reads=["win_kv"], writes=["Wkv"])

        def p1_tile(t, cs, qdest=None):
            B_ = (set0, set1, set2)[cs]
            tb, kb_, vb_ = ((7, 7, 5), (4, 4, 2), (1, 1, 0))[cs]
            if qdest is not None:
                tb, kb_, vb_ = 6, 6, None
            K_ = lambda n: "p1%s_%d" % (n, cs)
            xt, ss = B_["xt"], B_["ss"]
            DMA("sp", xt, xin[t * 128:(t + 1) * 128, :], writes=[K_("xt")])
            yield
            OP("act", "activation", [K_("xt")], [K_("ksq"), K_("ss")], out=B_["junk"], in_=xt, func=AF.Square, accum_out=ss[:, 0:1])
            yield
            OP("dve", "tensor_scalar", [K_("ss")], [K_("ss")], out=ss[:, 0:1], in0=ss[:, 0:1], scalar1=1.0 / 1024, scalar2=EPS, op0=ALU.mult, op1=ALU.add)
            yield
            OP("act", "activation", [K_("ss")], [K_("ss")], out=ss[:, 0:1], in_=ss[:, 0:1], func=AF.Ln)
            yield
            OP("act", "activation", [K_("ss")], [K_("ss")], out=ss[:, 0:1], in_=ss[:, 0:1], func=AF.Exp, scale=-0.5)
            yield
            OP("dve", "tensor_scalar", [K_("xt"), K_("ss")], [K_("xn")], out=B_["xn"], in0=xt, scalar1=ss[:, 0:1], scalar2=None, op0=ALU.mult)
            yield
            tp = pbf(tb)
            for c in range(8):
                OP("pe", "transpose", [K_("xn")], ["pb%d" % tb], out=tp[:, c * 128:(c + 1) * 128], in_=B_["xn"][:, c * 128:(c + 1) * 128], identity=identb[:])
            yield
            OP("dve", "tensor_tensor", ["pb%d" % tb, "pvs"], [K_("hT")], out=B_["hT"], in0=tp.rearrange("p (c t) -> p c t", c=8),
               in1=PVv("attn_g").unsqueeze(2).to_broadcast([128, 8, 128]), op=ALU.mult)
            yield
            if qdest is None:
                for c in range(8):
                    OP("pe", "matmul", [K_("hT"), "Wkv"], ["pb%d" % kb_], pb[kb_][:, :], lhsT=B_["hT"][:, c, :], rhs=Wkv[:, c, 0:512], start=(c == 0), stop=(c == 7))
                for c in range(8):
                    OP("pe", "matmul", [K_("hT"), "Wkv"], ["pb%d" % vb_], pb[vb_][:, :], lhsT=B_["hT"][:, c, :], rhs=Wkv[:, c, 512:1024], start=(c == 0), stop=(c == 7))
            else:
                for c in range(8):
                    OP("pe", "matmul", [K_("hT"), "Wq"], ["pb%d" % kb_], pb[kb_][:, :], lhsT=B_["hT"][:, c, :], rhs=qdest[2][:, c, :], start=(c == 0), stop=(c == 7))
            yield
            kf, ksq, ssk, kbb, rt = B_["kf"], B_["ksq"], B_["ssk"], B_["kbb"], B_["rt"]
            OP("act", "activation", ["pb%d" % kb_], [K_("kf")], out=kf, in_=pb[kb_][:, :], func=AF.Copy)
            yield
            if qdest is None:
                OP("act", "activation", ["pb%d" % vb_], ["V%d" % t], out=Vt[:, t, :], in_=pb[vb_][:, :], func=AF.Copy)
            OP("pool", "tensor_tensor", [K_("kf")], [K_("ksq")], out=ksq, in0=kf, in1=kf, op=ALU.mult)
            yield
            OP("dve", "tensor_reduce", [K_("ksq")], [K_("ssk")], out=ssk, in_=ksq.rearrange("p (g d) -> p g d", g=8), axis=AX.X, op=ALU.add)
            yield
            OP("dve", "tensor_scalar", [K_("ssk")], [K_("ssk")], out=ssk, in0=ssk, scalar1=1.0 / 64, scalar2=EPS, op0=ALU.mult, op1=ALU.add)
            yield
            OP("act", "activation", [K_("ssk")], [K_("ssk")], out=ssk, in_=ssk, func=AF.Ln)
            yield
            OP("act", "activation", [K_("ssk")], [K_("ssk")], out=ssk, in_=ssk, func=AF.Exp, scale=-0.5)
            yield
            kf3 = kf.rearrange("p (g d) -> p g d", g=8)
            OP("dve", "tensor_tensor", [K_("kf"), K_("ssk")], [K_("kf")], out=kf3, in0=kf3, in1=ssk.unsqueeze(2).to_broadcast([128, 8, 64]), op=ALU.mult)
            yield
            OP("dve", "tensor_tensor", [K_("kf"), "pvs"], [K_("kf")], out=kf3, in0=kf3, in1=PVv("k_g" if qdest is None else "q_g").unsqueeze(1).to_broadcast([128, 8, 64]), op=ALU.mult)
            yield
            kb3 = kbb.rearrange("p (g d) -> p g d", g=8)
            OP("act", "activation", [K_("kf")], [K_("kbb")], out=kbb, in_=kf, func=AF.Copy)
            cb = cosT[:, t, :].unsqueeze(1).to_broadcast([128, 8, 8])
            sn = sinT[:, t, :].unsqueeze(1).to_broadcast([128, 8, 8])
            x1, x2 = kf3[:, :, 0:8], kf3[:, :, 8:16]
            OP("dve", "tensor_tensor", [K_("kf")], [K_("rt0")], out=rt[:, 0], in0=x1, in1=cb, op=ALU.mult)
            OP("dve", "tensor_tensor", [K_("kf")], [K_("rt1")], out=rt[:, 1], in0=x2, in1=sn, op=ALU.mult)
            yield
            OP("dve", "tensor_tensor", [K_("kf")], [K_("rt2")], out=rt[:, 2], in0=x2, in1=cb, op=ALU.mult)
            OP("dve", "tensor_tensor", [K_("kf")], [K_("rt3")], out=rt[:, 3], in0=x1, in1=sn, op=ALU.mult)
            yield
            OP("dve", "tensor_tensor", [K_("rt0"), K_("rt1")], [K_("kbb")], out=kb3[:, :, 0:8], in0=rt[:, 0], in1=rt[:, 1], op=ALU.subtract)
            OP("dve", "tensor_tensor", [K_("rt2"), K_("rt3")], [K_("kbb")], out=kb3[:, :, 8:16], in0=rt[:, 2], in1=rt[:, 3], op=ALU.add)
            yield
            for h in range(4):
                OP("pe", "transpose", [K_("kbb")], ["pb%d" % tb], out=tp[:, h * 128:(h + 1) * 128], in_=kbb[:, h * 128:(h + 1) * 128], identity=identb[:])
            yield
            if qdest is None:
                OP("act", "activation", ["pb%d" % tb], ["KT%d" % t], out=KT[:, :, t * 128:(t + 1) * 128], in_=tp[:, 0:512].rearrange("p (h t) -> p h t", h=4), func=AF.Copy)
            else:
                OP("act", "activation", ["pb%d" % tb], [qdest[1]], out=qdest[0], in_=tp[:, 0:512].rearrange("p (h t) -> p h t", h=4), func=AF.Copy)
            yield

        def zip_run(*gens):
            gens = list(gens)
            while gens:
                for gnr in list(gens):
                    try:
                        next(gnr)
                    except StopIteration:
                        gens.remove(gnr)

        t = 0
        while t < NT:
            n3 = min(3, NT - t)
            zip_run(*[p1_tile(t + k, k) for k in range(n3)])
            t += n3

        P.barrier()
        apos[0] = mark_kv
        Wq = A([128, 8, 512], BF16)
        QT2 = [A([128, 4, 512], BF16), A([128, 4, 512], BF16)]
        ptb = [A([128, 512], BF16) for _ in range(6)]
        o0 = A([128, 512])
        o1 = A([128, 512])
        rl = A([128, 512])
        osq = A([128, 512])
        obt = [A([128, 512], BF16) for _ in range(2)]
        lacc = [[A([128, 512]), A([128, 512])], [A([128, 512]), A([128, 512])]]
        DMA("sp", Wq, winv[:, :, C_BQ:C_BQ + 512], reads=WIN, writes=["Wq"])
        kvkeys_k = ["KT%d" % t for t in range(NT)]
        spi = 0
        pti = 0
        obi = 0
        def q_setup(g):
            for i in range(4):
                t = NTC + 4 * g + i
                yield from p1_tile(t, 0, qdest=(QT2[g % 2][:, :, i * 128:(i + 1) * 128], "QT%d_%d" % (g % 2, i), Wq))

        for _ in q_setup(0):
            pass
        for g in range(NQG):
            QT = QT2[g % 2]
            qgen = q_setup(g + 1) if g + 1 < NQG else iter(())
            qkeys = ["QT%d_%d" % (g % 2, i) for i in range(4)]
            J = NTC + 4 * g + 4
            tiles = [(h, j) for h in range(4) for j in range(J)]
            tinfo = {}

            def emit_qk(n):
                nonlocal spi, pti
                h, j = tiles[n]
                banks = (0, 1) if spi % 2 == 0 else (4, 5)
                spi += 1
                for c in range(2):
                    OP("pe", "matmul", qkeys + ["KT%d" % j], ["pb%d" % banks[c]], pb[banks[c]][:, :],
                       lhsT=KT[64 * c:64 * c + 64, h, j * 128:(j + 1) * 128], rhs=QT[64 * c:64 * c + 64, h, :], start=True, stop=True)
                info = []
                for c in range(2):
                    pi = pti % 6
                    pti += 1
                    pk = "pt%d" % pi
                    bias = biasC if j < NTC else negshift
                    OP("act", "activation", ["pb%d" % banks[c], "biasC", "negshift"], [pk], out=ptb[pi], in_=pb[banks[c]][:, :], func=AF.Exp, bias=bias[:, 0:1], scale=0.125)
                    m = j - (NTC + 4 * g)
                    if m >= 0:
                        OP("pool", "tensor_tensor", [pk, "cmaskb"], [pk], out=ptb[pi], in0=ptb[pi], in1=cmaskb[:, m, :], op=ALU.mult)
                    info.append((pi, pk))
                tinfo[n] = info

            def emit_pv(n):
                h, j = tiles[n]
                info = tinfo.pop(n)
                for c in range(2):
                    pi, pk = info[c]
                    OP("pe", "matmul", [pk, "V%d" % j], ["pb%d" % (2 + c)], pb[2 + c][:, :], lhsT=Vt[:, j, h * 128:(h + 1) * 128], rhs=ptb[pi], start=(j == 0), stop=(j == J - 1))
                for c in range(2):
                    pi, pk = info[c]
                    le = "dve"
                    src_l, sk_l = lacc[c][j % 2], "lacc%d_%d" % (c, j % 2)
                    dst_l, dk_l = lacc[c][(j + 1) % 2], "lacc%d_%d" % (c, (j + 1) % 2)
                    if j == 0:
                        OP(le, "tensor_copy", [pk], [dk_l], out=dst_l, in_=ptb[pi])
                    else:
                        OP(le, "tensor_tensor", [pk, sk_l], [dk_l], out=dst_l, in0=src_l, in1=ptb[pi], op=ALU.add)
                if j == J - 1:
                    emit_post(h)

            def emit_post(h):
                nonlocal obi
                OP("pe", "matmul", ["lacc0_%d" % (J % 2)], ["pb7"], pb[7][:, :], lhsT=onesf[:], rhs=lacc[0][J % 2], start=True, stop=True)
                OP("act", "activation", ["pb7"], ["rl"], out=rl, in_=pb[7][:, :], func=AF.Ln)
                OP("act", "activation", ["rl"], ["rl"], out=rl, in_=rl, func=AF.Exp, scale=-1.0)
                OP("pe", "matmul", ["lacc1_%d" % (J % 2)], ["pb7"], pb[7][:, :], lhsT=onesf[:], rhs=lacc[1][J % 2], start=True, stop=True)
                OP("dve", "tensor_tensor", ["pb2", "rl"], ["o0"], out=o0, in0=pb[2][:, :], in1=rl, op=ALU.mult)
                OP("act", "activation", ["pb7"], ["rl"], out=rl, in_=pb[7][:, :], func=AF.Ln)
                OP("act", "activation", ["rl"], ["rl"], out=rl, in_=rl, func=AF.Exp, scale=-1.0)
                OP("dve", "tensor_tensor", ["pb3", "rl"], ["o1"], out=o1, in0=pb[3][:, :], in1=rl, op=ALU.mult)
                OP("dve", "scalar_tensor_tensor", ["o0", "o1", "neglam"], ["o0"], out=o0, in0=o1, scalar=neglam[:, 0:1], in1=o0, op0=ALU.mult, op1=ALU.add)
                OP("pool", "tensor_tensor", ["o0"], ["osq"], out=osq, in0=o0, in1=o0, op=ALU.mult)
                pending.append((nnow[0] + 6, h))

            def emit_post_b(h):
                nonlocal obi
                OP("pe", "matmul", ["osq", "onesf"], ["pb7"], pb[7][:, :], lhsT=onesf[:], rhs=osq, start=True, stop=True)
                OP("dve", "tensor_scalar", ["pb7"], ["rl"], out=rl, in0=pb[7][:, :], scalar1=1.0 / 128, scalar2=EPS, op0=ALU.mult, op1=ALU.add)
                OP("act", "activation", ["rl"], ["rl"], out=rl, in_=rl, func=AF.Ln)
                OP("act", "activation", ["rl"], ["rl"], out=rl, in_=rl, func=AF.Exp, scale=-0.5)
                ob = obt[obi % 2]
                obk = "obt%d" % (obi % 2)
                obi += 1
                OP("dve", "scalar_tensor_tensor", ["o0", "gscale", "rl"], [obk], out=ob, in0=o0, scalar=gscale[:, 0:1], in1=rl, op0=ALU.mult, op1=ALU.mult)
                DMA("sp", obT_d[h * 128:(h + 1) * 128, g * 512:(g + 1) * 512], ob, reads=[obk], writes=["obT_d"])

            LA = 1
            pending = []
            nnow = [0]
            for n in range(len(tiles) + LA):
                nnow[0] = n
                if n < len(tiles):
                    emit_qk(n)
                if n - LA >= 0:
                    emit_pv(n - LA)
                next(qgen, None)
                while pending and pending[0][0] <= n:
                    emit_post_b(pending.pop(0)[1])
            while pending:
                emit_post_b(pending.pop(0)[1])
            for _ in qgen:
                pass

        P.barrier()
        areset()
        Wg = A([128, 8, 1536], BF16)
        Wz = A([128, 8, 512], BF16)
        Wab = A([128, 8, 8], BF16)
        xtb = [A([128, 1024])]
        sq = A([128, 512])
        junk = sq.bitcast(BF16)
        rn = A([128, 512])
        xn = A([128, 1024], BF16)
        ssx = A([128, 1])
        hT4 = A([128, 8, 512], BF16)
        pbuf = [A([128, 516]), A([128, 516])]
        halo = A([128, 12, 4])
        cv = A([128, 12, 512], BF16)
        qTg = A([128, 4, 512], BF16)
        kTg = A([128, 4, 512], BF16)
        vTg = A([128, 4, 512], BF16)
        S = A([128, 4, 128])
        Sb = A([128, 4, 128], BF16)
        SMN = ("xa", "gt", "beta", "nbeta", "Gc", "eG", "bg", "dgl", "ekd")
        F3N = ("diagG", "GSs", "EGb", "D", "Dm", "Ds", "L0")
        B3N = ("ktm", "vtm", "vb", "kbg", "X0", "X1", "Y0", "Y1", "R0", "R1", "Aa")
        smS = [{n: A([128, 4]) for n in SMN} for _ in range(2)]
        f3S = [{n: A([128, 4, 128]) for n in F3N} for _ in range(2)]
        b3S = [{n: A([128, 4, 128], BF16) for n in B3N} for _ in range(2)]
        HB = [{"u": A([128, 4, 128]), "nwT": A([128, 4, 128], BF16), "qd": A([128, 4, 128], BF16), "aT": A([128, 4, 128], BF16),
               "kdec": A([128, 4, 128], BF16), "egl": A([128, 8]), "sz": A([128, 4, 128], BF16)} for _ in range(4)]
        rsm = A([128, 4])
        rf3 = {n: A([128, 4, 128]) for n in ("osb", "osq", "on")}
        rb3 = {n: A([128, 4, 128], BF16) for n in ("vnew", "oab", "oaTt")}
        DMA("sp", Wg, winv[:, :, C_AQ:C_AQ + 1536], reads=WIN, writes=["Wg"])
        DMA("sp", Wz, winv[:, :, C_AZ:C_AZ + 512], reads=WIN, writes=["Wz"])
        with nc.allow_non_contiguous_dma(reason="tiny a/b gate columns"):
            DMA("sp", Wab, winv[:, :, C_AA:C_AA + 8], reads=WIN, writes=["Wab"])
        OP("pool", "memset", [], ["halo"], halo.rearrange("p a b -> p (a b)"), 0.0)
        OP("pool", "memset", [], ["S"], S.rearrange("p a b -> p (a b)"), 0.0)
        OP("pool", "memset", [], ["Sb"], Sb.rearrange("p a b -> p (a b)"), 0.0)
        identf = C("ident")

        def bc_h(ap2d):
            return ap2d.unsqueeze(1).to_broadcast([128, 4, 128])

        def bc_l(ap2d):
            return ap2d.unsqueeze(2).to_broadcast([128, 4, 128])

        def v3(bank):
            return pb[bank][:, :].rearrange("p (h c) -> p h c", h=4)

        def tr4(src_fn, bank_half, dst, dkey, skeys):
            tp = pbf(3)
            o = 512 * bank_half
            for h in range(4):
                OP("pe", "transpose", skeys, ["pb3"], out=tp[:, o + h * 128:o + (h + 1) * 128], in_=src_fn(h), identity=identb[:])
            OP("act", "activation", ["pb3"], [dkey], out=dst, in_=tp[:, o:o + 512].rearrange("p (h t) -> p h t", h=4), func=AF.Copy)

        altb = [0]

        def nb():
            altb[0] ^= 1
            return 4 + altb[0]

        def group_level(G):
            hkeys = ["hT4_%d" % i for i in range(4)]
            for i in range(4):
                t = 4 * G + i
                load_norm_T(xin[t * 128:(t + 1) * 128, :], "attn_g", hT4[:, :, i * 128:(i + 1) * 128], hkeys[i], xtb, junk, xn, ssx, junk_key="sq")
            co, _ = PV["convw"]
            for cc in range(12):
                bk = nb()
                for c in range(8):
                    OP("pe", "matmul", hkeys + ["Wg"], ["pb%d" % bk], pb[bk][:, :], lhsT=Wg[:, c, cc * 128:(cc + 1) * 128], rhs=hT4[:, c, :], start=(c == 0), stop=(c == 7))
                pbu, pk, ck = pbuf[cc % 2], "pbuf%d" % (cc % 2), "cv%d" % cc
                OP("pool", "tensor_copy", ["halo"], [pk], out=pbu[:, 0:4], in_=halo[:, cc, :])
                OP("act", "activation", ["pb%d" % bk], [pk], out=pbu[:, 4:516], in_=pb[bk][:, :], func=AF.Copy)
                wcol = lambda j: pvs[:, co + cc * 4 + j:co + cc * 4 + j + 1]
                OP("dve", "tensor_scalar", [pk, "pvs"], ["rn"], out=rn, in0=pbu[:, 1:513], scalar1=wcol(0), scalar2=None, op0=ALU.mult)
                for j in range(1, 4):
                    OP("dve", "scalar_tensor_tensor", [pk, "rn", "pvs"], ["rn"], out=rn, in0=pbu[:, 1 + j:513 + j], scalar=wcol(j), in1=rn, op0=ALU.mult, op1=ALU.add)
                OP("pool", "tensor_copy", [pk], ["halo"], out=halo[:, cc, :], in_=pbu[:, 512:516])
                OP("act", "activation", ["rn"], [ck], out=cv[:, cc, :], in_=rn, func=AF.Silu)
            for cc in range(8):
                ck = "cv%d" % cc
                OP("pool", "tensor_tensor", [ck], ["sq"], out=sq, in0=cv[:, cc, :], in1=cv[:, cc, :], op=ALU.mult)
                bk2 = nb()
                OP("pe", "matmul", ["sq"], ["pb%d" % bk2], pb[bk2][:, :], lhsT=onesf[:], rhs=sq, start=True, stop=True)
                OP("dve", "tensor_scalar", ["pb%d" % bk2], ["rn"], out=rn, in0=pb[bk2][:, :], scalar1=EPS, scalar2=None, op0=ALU.add)
                OP("act", "activation", ["rn"], ["rn"], out=rn, in_=rn, func=AF.Ln)
                OP("act", "activation", ["rn"], ["rn"], out=rn, in_=rn, func=AF.Exp, scale=-0.5)
                if cc < 4:
                    OP("dve", "scalar_tensor_tensor", [ck, "rn"], ["qTg%d" % cc], out=qTg[:, cc, :], in0=cv[:, cc, :], scalar=128.0 ** -0.5, in1=rn, op0=ALU.mult, op1=ALU.mult)
                else:
                    OP("dve", "tensor_tensor", [ck, "rn"], ["kTg%d" % (cc - 4)], out=kTg[:, cc - 4, :], in0=cv[:, cc, :], in1=rn, op=ALU.mult)
            for h in range(4):
                OP("pool", "tensor_copy", ["cv%d" % (8 + h)], ["vTg%d" % h], out=vTg[:, h, :], in_=cv[:, 8 + h, :])

        hkeys = ["hT4_%d" % i for i in range(4)]
        QK_ = ["qTg%d" % h for h in range(4)]
        KK_ = ["kTg%d" % h for h in range(4)]
        VK_ = ["vTg%d" % h for h in range(4)]

        def prep(G, i):
            t = 4 * G + i
            own = t >= NTC
            ps = i % 2
            hs = i % 4
            cols = slice(i * 128, (i + 1) * 128)
            sm, f3, b3, hb_ = smS[ps], f3S[ps], b3S[ps], HB[hs]
            K_ = lambda n: "%s_%d" % (n, ps)
            H_ = lambda n: "%s_h%d" % (n, hs)
            pair = (4, 5) if ps == 0 else (0, 2)
            tog = [0]

            def nbp():
                tog[0] ^= 1
                return pair[tog[0]]
            so = 64 * ps
            ab = pb[1][:, so:so + 8]
            if own:
                zb = nbp()
                for c in range(8):
                    OP("pe", "matmul", [hkeys[i], "Wz"], ["pb%d" % zb], pb[zb][:, :], lhsT=hT4[:, c, cols], rhs=Wz[:, c, :], start=(c == 0), stop=(c == 7))
            for c in range(8):
                OP("pe", "matmul", [hkeys[i], "Wab"], ["pb1"], ab, lhsT=hT4[:, c, cols], rhs=Wab[:, c, :], start=(c == 0), stop=(c == 7))
            yield
            if own:
                OP("act", "activation", ["pb%d" % zb], [H_("sz")], out=hb_["sz"], in_=v3(zb), func=AF.Silu)
            OP("dve", "tensor_tensor", ["pb1", "pvs"], [K_("xa")], out=sm["xa"], in0=pb[1][:, so:so + 4], in1=PVv("dt_b"), op=ALU.add)
            OP("act", "activation", ["pb1"], [K_("beta")], out=sm["beta"], in_=pb[1][:, so + 4:so + 8], func=AF.Exp, scale=-1.0)
            yield
            OP("act", "activation", [K_("xa")], [K_("xa")], out=sm["xa"], in_=sm["xa"], func=AF.Exp)
            OP("dve", "tensor_scalar", [K_("beta")], [K_("beta")], out=sm["beta"], in0=sm["beta"], scalar1=1.0, scalar2=None, op0=ALU.add)
            yield
            OP("act", "activation", [K_("xa")], [K_("xa")], out=sm["xa"], in_=sm["xa"], func=AF.Ln, bias=oneT[:, 0:1], scale=1.0)
            OP("dve", "reciprocal", [K_("beta")], [K_("beta")], out=sm["beta"], in_=sm["beta"])
            yield
            OP("dve", "tensor_tensor", [K_("xa"), "negA"], [K_("gt")], out=sm["gt"], in0=sm["xa"], in1=negA[:], op=ALU.mult)
            OP("dve", "tensor_scalar", [K_("beta")], [K_("nbeta")], out=sm["nbeta"], in0=sm["beta"], scalar1=-1.0, scalar2=None, op0=ALU.mult)
            yield
            OP("pe", "matmul", [K_("gt")], ["pb1"], pb[1][:, so + 16:so + 20], lhsT=C("tri"), rhs=sm["gt"], start=True, stop=True)
            yield
            OP("dve", "tensor_copy", ["pb1"], [K_("Gc")], out=sm["Gc"], in_=pb[1][:, so + 16:so + 20])
            yield
            OP("pe", "matmul", [K_("Gc")], ["pb1"], pb[1][:, so + 32:so + 36], lhsT=C("sellast"), rhs=sm["Gc"], start=True, stop=True)
            OP("pe", "matmul", [K_("Gc")], ["pb1"], pb[1][:, so + 36:so + 40], lhsT=C("sel63"), rhs=sm["Gc"], start=True, stop=True)
            OP("pe", "matmul", [K_("Gc")], ["pb1"], pb[1][:, so + 40:so + 44], lhsT=C("sel127"), rhs=sm["Gc"], start=True, stop=True)
            OP("act", "activation", [K_("Gc")], [K_("eG")], out=sm["eG"], in_=sm["Gc"], func=AF.Exp)
            OP("dve", "tensor_tensor", [K_("Gc")], [K_("diagG")], out=f3["diagG"], in0=bc_h(identf), in1=bc_l(sm["Gc"]), op=ALU.mult)
            yield
            OP("dve", "tensor_tensor", ["pb1", K_("Gc")], [K_("dgl")], out=sm["dgl"], in0=pb[1][:, so + 32:so + 36], in1=sm["Gc"], op=ALU.subtract)
            OP("act", "activation", ["pb1"], [H_("egl")], out=hb_["egl"], in_=pb[1][:, so + 36:so + 44], func=AF.Exp)
            gsb = nbp()
            OP("pe", "matmul", [K_("diagG")], ["pb%d" % gsb], pb[gsb][:, :], lhsT=onesf[:], rhs=f3["diagG"].rearrange("p h c -> p (h c)"), start=True, stop=True)
            yield
            OP("act", "activation", [K_("dgl")], [K_("ekd")], out=sm["ekd"], in_=sm["dgl"], func=AF.Exp)
            OP("dve", "tensor_tensor", [K_("beta"), K_("eG")], [K_("bg")], out=sm["bg"], in0=sm["beta"], in1=sm["eG"], op=ALU.mult)
            OP("act", "activation", ["pb%d" % gsb], [K_("GSs")], out=f3["GSs"], in_=v3(gsb), func=AF.Copy)
            yield
            OP("act", "activation", [K_("GSs")], [K_("EGb")], out=f3["EGb"], in_=f3["GSs"], func=AF.Exp)
            OP("dve", "tensor_tensor", [K_("GSs"), K_("Gc")], [K_("D")], out=f3["D"], in0=f3["GSs"], in1=bc_l(sm["Gc"]), op=ALU.subtract)
            yield
            OP("dve", "tensor_scalar_max", [K_("D")], [K_("D")], out=f3["D"], in0=f3["D"], scalar1=0.0)
            yield
            OP("act", "activation", [K_("D")], [K_("D")], out=f3["D"], in_=f3["D"], func=AF.Exp, scale=-1.0)
            tr4(lambda h: kTg[:, h, cols], 0, b3["ktm"], K_("ktm"), KK_)
            yield
            OP("pool", "tensor_tensor", [K_("D")], [K_("Dm")], out=f3["Dm"], in0=f3["D"], in1=bc_h(C("mincl")), op=ALU.mult)
            OP("pool", "tensor_tensor", [K_("D")], [K_("Ds")], out=f3["Ds"], in0=f3["D"], in1=bc_h(C("mstrict")), op=ALU.mult)
            tr4(lambda h: vTg[:, h, cols], 1, b3["vtm"], K_("vtm"), VK_)
            yield
            OP("dve", "tensor_tensor", [K_("vtm"), K_("beta")], [K_("vb")], out=b3["vb"], in0=b3["vtm"], in1=bc_l(sm["beta"]), op=ALU.mult)
            OP("pool", "tensor_tensor", [K_("ktm"), K_("bg")], [K_("kbg")], out=b3["kbg"], in0=b3["ktm"], in1=bc_l(sm["bg"]), op=ALU.mult)
            OP("pool", "tensor_tensor", [K_("ktm"), K_("ekd")], [H_("kdec")], out=hb_["kdec"], in0=b3["ktm"], in1=bc_l(sm["ekd"]), op=ALU.mult)
            bk = nbp()
            for h in range(4):
                OP("pe", "matmul", KK_, ["pb%d" % bk], pb[bk][:, h * 128:(h + 1) * 128], lhsT=kTg[:, h, cols], rhs=kTg[:, h, cols], start=True, stop=True)
            yield
            OP("dve", "tensor_tensor", ["pb%d" % bk, K_("Ds")], [K_("L0")], out=f3["L0"], in0=v3(bk), in1=f3["Ds"], op=ALU.mult)
            bk = nbp()
            for h in range(4):
                OP("pe", "matmul", KK_ + QK_, ["pb%d" % bk], pb[bk][:, h * 128:(h + 1) * 128], lhsT=qTg[:, h, cols], rhs=kTg[:, h, cols], start=True, stop=True)
            yield
            OP("dve", "tensor_tensor", [K_("L0"), K_("nbeta")], [K_("X0")], out=b3["X0"], in0=f3["L0"], in1=bc_l(sm["nbeta"]), op=ALU.mult)
            OP("dve", "tensor_tensor", ["pb%d" % bk, K_("Dm")], [K_("Aa")], out=b3["Aa"], in0=v3(bk), in1=f3["Dm"], op=ALU.mult)
            yield
            tr4(lambda h: b3["X0"][:, h, :], 0, b3["Y0"], K_("Y0"), [K_("X0")])
            yield
            tr4(lambda h: b3["Aa"][:, h, :], 1, hb_["aT"], H_("aT"), [K_("Aa")])
            OP("dve", "tensor_tensor", QK_ + [K_("EGb")], [H_("qd")], out=hb_["qd"], in0=qTg[:, :, cols], in1=f3["EGb"], op=ALU.mult)
            yield
            OP("dve", "tensor_tensor", [K_("Y0"), "identb"], [K_("R0")], out=b3["R0"], in0=b3["Y0"], in1=bc_h(identb[:]), op=ALU.add)
            for k in range(5):
                Xk, Yk, Rk = "X%d" % (k % 2), "Y%d" % (k % 2), "R%d" % (k % 2)
                Xn, Yn, Rn = "X%d" % ((k + 1) % 2), "Y%d" % ((k + 1) % 2), "R%d" % ((k + 1) % 2)
                bk = nbp()
                for h in range(4):
                    OP("pe", "matmul", [K_(Xk), K_(Yk)], ["pb%d" % bk], pb[bk][:, h * 128:(h + 1) * 128], lhsT=b3[Yk][:, h, :], rhs=b3[Xk][:, h, :], start=True, stop=True)
                bk2 = nbp()
                if k < 4:
                    for h in range(4):
                        OP("pe", "matmul", [K_(Xk), K_(Yk)], ["pb%d" % bk2], pb[bk2][:, h * 128:(h + 1) * 128], lhsT=b3[Xk][:, h, :], rhs=b3[Yk][:, h, :], start=True, stop=True)
                yield
                OP("act", "activation", ["pb%d" % bk], [K_(Xn)], out=b3[Xn], in_=v3(bk), func=AF.Copy)
                if k < 4:
                    OP("dve", "tensor_copy", ["pb%d" % bk2], [K_(Yn)], out=b3[Yn], in_=v3(bk2))
                yield
                bk = nbp()
                for h in range(4):
                    OP("pe", "matmul", [K_(Xn), K_(Rk)], ["pb%d" % bk], pb[bk][:, h * 128:(h + 1) * 128], lhsT=b3[Xn][:, h, :], rhs=b3[Rk][:, h, :], start=True, stop=True)
                yield
                OP("dve", "tensor_tensor", ["pb%d" % bk, K_(Rk)], [K_(Rn)], out=b3[Rn], in0=v3(bk), in1=b3[Rk], op=ALU.add)
                yield
            R5 = "R1"
            bk = nbp()
            for h in range(4):
                OP("pe", "matmul", [K_(R5), K_("vb")], ["pb%d" % bk], pb[bk][:, h * 128:(h + 1) * 128], lhsT=b3[R5][:, h, :], rhs=b3["vb"][:, h, :], start=True, stop=True)
            bk2 = nbp()
            for h in range(4):
                OP("pe", "matmul", [K_(R5), K_("kbg")], ["pb%d" % bk2], pb[bk2][:, h * 128:(h + 1) * 128], lhsT=b3["kbg"][:, h, :], rhs=b3[R5][:, h, :], start=True, stop=True)
            yield
            OP("act", "activation", ["pb%d" % bk], [H_("u")], out=hb_["u"], in_=v3(bk), func=AF.Copy)
            OP("dve", "tensor_scalar", ["pb%d" % bk2], [H_("nwT")], out=hb_["nwT"], in0=v3(bk2), scalar1=-1.0, scalar2=None, op0=ALU.mult)
            yield

        def rec(G, i):
            t = 4 * G + i
            own = t >= NTC
            hs = i % 4
            hb_ = HB[hs]
            H_ = lambda n: "%s_h%d" % (n, hs)
            for ch in range(2):
                r = slice(64 * ch, 64 * ch + 64)
                bk = 6
                for h in range(4):
                    OP("pe", "matmul", [H_("nwT"), "Sb"], ["pb%d" % bk], pb[bk][r, h * 128:(h + 1) * 128], lhsT=hb_["nwT"][:, h, r], rhs=Sb[:, h, :], start=True, stop=True)
                yield
                OP("dve", "tensor_tensor", ["pb%d" % bk, H_("u")], ["vnew"], out=rb3["vnew"][r], in0=v3(bk)[r], in1=hb_["u"][r], op=ALU.add)
                yield
                for h in range(4):
                    OP("pe", "matmul", [H_("kdec"), "vnew"], ["pb%d" % bk], pb[bk][:, h * 128:(h + 1) * 128], lhsT=hb_["kdec"][r, h, :], rhs=rb3["vnew"][r, h, :], start=True, stop=True)
                if own:
                    for h in range(4):
                        OP("pe", "matmul", [H_("qd"), "Sb"], ["pb7"], pb[7][r, h * 128:(h + 1) * 128], lhsT=hb_["qd"][:, h, r], rhs=Sb[:, h, :], start=True, stop=False)
                        OP("pe", "matmul", [H_("aT"), "vnew"], ["pb7"], pb[7][r, h * 128:(h + 1) * 128], lhsT=hb_["aT"][r, h, r], rhs=rb3["vnew"][r, h, :], start=False, stop=True)
                OP("dve", "tensor_tensor", ["S", H_("egl")], ["S"], out=S, in0=S, in1=bc_l(hb_["egl"][:, 4 * ch:4 * ch + 4]), op=ALU.mult)
                yield
                OP("dve", "tensor_tensor", ["S", "pb%d" % bk], ["S"], out=S, in0=S, in1=v3(bk), op=ALU.add)
                yield
                OP("act", "activation", ["S"], ["Sb"], out=Sb, in_=S, func=AF.Copy)
                yield
            if own:
                OP("act", "activation", ["pb7"], ["osb"], out=rf3["osb"], in_=v3(7), func=AF.Copy)
                yield
                OP("pool", "tensor_tensor", ["osb"], ["osq"], out=rf3["osq"], in0=rf3["osb"], in1=rf3["osb"], op=ALU.mult)
                yield
                OP("dve", "tensor_reduce", ["osq"], ["ssq"], out=rsm, in_=rf3["osq"], axis=AX.X, op=ALU.add)
                yield
                OP("dve", "tensor_scalar", ["ssq"], ["ssq"], out=rsm, in0=rsm, scalar1=1.0 / 128, scalar2=EPS, op0=ALU.mult, op1=ALU.add)
                yield
                OP("act", "activation", ["ssq"], ["ssq"], out=rsm, in_=rsm, func=AF.Ln)
                yield
                OP("act", "activation", ["ssq"], ["ssq"], out=rsm, in_=rsm, func=AF.Exp, scale=-0.5)
                yield
                OP("dve", "tensor_tensor", ["osb", "ssq"], ["on"], out=rf3["on"], in0=rf3["osb"], in1=bc_l(rsm), op=ALU.mult)
                yield
                OP("dve", "tensor_tensor", ["on", "pvs"], ["on"], out=rf3["on"], in0=rf3["on"], in1=bc_h(PVv("gdn_g")), op=ALU.mult)
                yield
                OP("dve", "tensor_tensor", ["on", H_("sz")], ["oab"], out=rb3["oab"], in0=rf3["on"], in1=hb_["sz"], op=ALU.mult)
                yield
                tr4(lambda h: rb3["oab"][:, h, :], 0, rb3["oaTt"], "oaTt", ["oab"])
                tok0 = (t - NTC) * 128
                DMA("sp", oaT_d.rearrange("(h p) t -> p h t", p=128)[:, :, tok0:tok0 + 128], rb3["oaTt"], reads=["oaTt"], writes=["oaT_d"])
                yield

        def chain(*gens):
            for gnr in gens:
                yield from gnr

        def zip_run(*gens):
            gens = list(gens)
            while gens:
                for gnr in list(gens):
                    try:
                        next(gnr)
                    except StopIteration:
                        gens.remove(gnr)

        for G in range(NG):
            group_level(G)
            if G == 0:
                zip_run(prep(G, 0), prep(G, 1))
            else:
                zip_run(prep(G, 0), prep(G, 1), chain(rec(G - 1, 2), rec(G - 1, 3)))
            zip_run(prep(G, 2), prep(G, 3), chain(rec(G, 0), rec(G, 1)))
        zip_run(chain(rec(NG - 1, 2), rec(NG - 1, 3)))

        P.barrier()
        areset()
        xtok = A([128, 4, 1024])
        junk = A([128, 1024], BF16)
        xn = A([128, 1024], BF16)
        ssx = A([128, 1])
        hT4 = A([128, 8, 512], BF16)
        oaT4 = A([128, 4, 512], BF16)
        obT4 = A([128, 4, 512], BF16)
        mT = A([128, 8, 512], BF16)
        upT = A([128, 32, 512], BF16)
        pT = A([128, 2, 512], BF16)
        ptile = [A([128, 256]), A([128, 256])]
        pbf16 = [A([128, 256], BF16), A([128, 256], BF16)]
        sga = [A([128, 512]), A([128, 512])]
        sgb = [A([128, 512]), A([128, 512])]
        m1 = [A([128, 512]), A([128, 512])]
        m2 = [A([128, 512]), A([128, 512])]
        rl_ = [A([128, 512]), A([128, 512])]
        NWB = 6
        wbuf = [A([128, 8, 512], BF16) for _ in range(NWB)]
        wbi = [0]
        WOUT = ["wout_b0", "wout_b1"]
        WPG = ["wpg_b0", "wpg_b1"]
        WUP = ["wup_b%d" % r for r in range(8)]
        WDN = ["wdn_b%d" % r for r in range(8)]

        def wload(src_ap, nk, rkeys):
            i = wbi[0] % NWB
            wbi[0] += 1
            DMA("sp", wbuf[i][:, 0:nk, :], src_ap, reads=rkeys, writes=["wb%d" % i])
            return wbuf[i], "wb%d" % i

        rr = [0]

        def rbank(excl=()):
            while True:
                rr[0] = (rr[0] + 1) % 8
                if rr[0] not in excl:
                    return rr[0]

        def v2(name, shape_rows):
            return name.rearrange("(c p) n -> p c n", p=128)

        wgv = winv
        woav, wobv = v2(woa_b, 512), v2(wob_b, 512)
        woutv, wupv, wdnv, wpgv, wplev = v2(wout_b, 0), v2(wup_b, 0), v2(wdn_b, 0), v2(wpg_b, 0), v2(wple_b, 0)
        for g in range(NQG):
            tok0 = g * 512
            hk = ["hT4_%d" % i for i in range(4)]
            for i in range(4):
                t = NTC + 4 * g + i
                DMA("sp", xtok[:, i, :], xin[t * 128:(t + 1) * 128, :], writes=["xtok%d" % i])
                load_norm_T(None, "attn_g", hT4[:, :, i * 128:(i + 1) * 128], hk[i], None, junk, xn, ssx, keep_x=(xtok[:, i, :], "xtok%d" % i))
            DMA("sp", oaT4, oaT_d.rearrange("(c p) t -> p c t", p=128)[:, :, tok0:tok0 + 512], reads=["oaT_d"], writes=["oaT4"])
            DMA("sp", obT4, obT_d.rearrange("(c p) t -> p c t", p=128)[:, :, tok0:tok0 + 512], reads=["obT_d"], writes=["obT4"])
            for i in range(4):
                pt_, pb_ = ptile[i % 2], pbf16[i % 2]
                DMA("sp", pt_, pin[tok0 + i * 128:tok0 + (i + 1) * 128, :], writes=["ptile%d" % (i % 2)])
                OP("act", "activation", ["ptile%d" % (i % 2)], ["pbf%d" % (i % 2)], out=pb_, in_=pt_, func=AF.Copy)
                tp = pbf(6)
                for c in range(2):
                    OP("pe", "transpose", ["pbf%d" % (i % 2)], ["pb6"], out=tp[:, c * 128:(c + 1) * 128], in_=pb_[:, c * 128:(c + 1) * 128], identity=identb[:])
                OP("act", "activation", ["pb6"], ["pT%d" % i], out=pT[:, :, i * 128:(i + 1) * 128], in_=tp[:, 0:256].rearrange("p (c t) -> p c t", c=2), func=AF.Copy)
            jj = 0
            for cb in range(2):
                wga, kga = wload(wgv[:, :, C_GA + cb * 512:C_GA + (cb + 1) * 512], 8, WIN)
                wgb, kgb = wload(wgv[:, :, C_GB + cb * 512:C_GB + (cb + 1) * 512], 8, WIN)
                woa, koa = wload(woav[:, :, cb * 512:(cb + 1) * 512], 4, ["woa_b"])
                wob, kob = wload(wobv[:, :, cb * 512:(cb + 1) * 512], 4, ["wob_b"])
                for j in range(4):
                    bs = [0, 1, 2, 3] if jj % 2 == 0 else [4, 5, 6, 7]
                    par = jj % 2
                    jj += 1
                    cs_ = slice(j * 128, (j + 1) * 128)
                    for c in range(8):
                        OP("pe", "matmul", hk + [kga], ["pb%d" % bs[0]], pb[bs[0]][:, :], lhsT=wga[:, c, cs_], rhs=hT4[:, c, :], start=(c == 0), stop=(c == 7))
                    for c in range(8):
                        OP("pe", "matmul", hk + [kgb], ["pb%d" % bs[1]], pb[bs[1]][:, :], lhsT=wgb[:, c, cs_], rhs=hT4[:, c, :], start=(c == 0), stop=(c == 7))
                    for c in range(4):
                        OP("pe", "matmul", ["oaT4", koa], ["pb%d" % bs[2]], pb[bs[2]][:, :], lhsT=woa[:, c, cs_], rhs=oaT4[:, c, :], start=(c == 0), stop=(c == 3))
                    for c in range(4):
                        OP("pe", "matmul", ["obT4", kob], ["pb%d" % bs[3]], pb[bs[3]][:, :], lhsT=wob[:, c, cs_], rhs=obT4[:, c, :], start=(c == 0), stop=(c == 3))
                    OP("act", "activation", ["pb%d" % bs[0]], ["sga%d" % par], out=sga[par], in_=pb[bs[0]][:, :], func=AF.Sigmoid)
                    OP("act", "activation", ["pb%d" % bs[1]], ["sgb%d" % par], out=sgb[par], in_=pb[bs[1]][:, :], func=AF.Sigmoid)
                    OP("dve", "tensor_tensor", ["pb%d" % bs[2], "sga%d" % par], ["m1%d" % par], out=m1[par], in0=pb[bs[2]][:, :], in1=sga[par], op=ALU.mult)
                    OP("dve", "tensor_tensor", ["pb%d" % bs[3], "sgb%d" % par], ["m2%d" % par], out=m2[par], in0=pb[bs[3]][:, :], in1=sgb[par], op=ALU.mult)
                    OP("pool", "tensor_tensor", ["m1%d" % par, "m2%d" % par], ["mT%d" % (cb * 4 + j)], out=mT[:, cb * 4 + j, :], in0=m1[par], in1=m2[par], op=ALU.add)
            MT = ["mT%d" % c for c in range(8)]
            for half in range(2):
                hs = slice(half * 512, (half + 1) * 512)
                wo, ko = wload(woutv[:, :, hs], 8, WOUT)
                for i in range(4):
                    bk = rbank()
                    for c in range(8):
                        OP("pe", "matmul", MT + [ko], ["pb%d" % bk], pb[bk][:, :], lhsT=mT[:, c, i * 128:(i + 1) * 128], rhs=wo[:, c, :], start=(c == 0), stop=(c == 7))
                    OP("dve", "tensor_tensor", ["pb%d" % bk, "xtok%d" % i], ["xtok%d" % i], out=xtok[:, i, hs], in0=pb[bk][:, :], in1=xtok[:, i, hs], op=ALU.add)
            for i in range(4):
                load_norm_T(None, "mlp_g", hT4[:, :, i * 128:(i + 1) * 128], hk[i], None, junk, xn, ssx, keep_x=(xtok[:, i, :], "xtok%d" % i))
            ui = 0
            for fb in range(8):
                wu, ku = wload(wupv[:, :, fb * 512:(fb + 1) * 512], 8, WUP)
                for j in range(4):
                    bk = rbank()
                    for c in range(8):
                        OP("pe", "matmul", hk + [ku], ["pb%d" % bk], pb[bk][:, :], lhsT=wu[:, c, j * 128:(j + 1) * 128], rhs=hT4[:, c, :], start=(c == 0), stop=(c == 7))
                    rb = rl_[ui % 2]
                    rk = "rl%d" % (ui % 2)
                    ui += 1
                    OP("act", "activation", ["pb%d" % bk], [rk], out=rb, in_=pb[bk][:, :], func=AF.Relu)
                    OP("pool", "tensor_tensor", [rk], ["upT%d" % (fb * 4 + j)], out=upT[:, fb * 4 + j, :], in0=rb, in1=rb, op=ALU.mult)
            for half in range(2):
                hs = slice(half * 512, (half + 1) * 512)
                bs = [0, 1, 2, 3] if half == 0 else [4, 5, 6, 7]
                for fq in range(4):
                    wd, kd = wload(wdnv[:, fq * 8:(fq + 1) * 8, hs], 8, WDN)
                    for i in range(4):
                        for c in range(8):
                            f = fq * 8 + c
                            OP("pe", "matmul", ["upT%d" % f, kd], ["pb%d" % bs[i]], pb[bs[i]][:, :], lhsT=upT[:, f, i * 128:(i + 1) * 128], rhs=wd[:, c, :], start=(f == 0), stop=(f == 31))
                for i in range(4):
                    OP("dve", "tensor_tensor", ["pb%d" % bs[i], "xtok%d" % i], ["xtok%d" % i], out=xtok[:, i, hs], in0=pb[bs[i]][:, :], in1=xtok[:, i, hs], op=ALU.add)
            for i in range(4):
                load_norm_T(None, "ple_g", hT4[:, :, i * 128:(i + 1) * 128], hk[i], None, junk, xn, ssx, keep_x=(xtok[:, i, :], "xtok%d" % i))
            gi = 0
            for half in range(2):
                hs = slice(half * 512, (half + 1) * 512)
                wp, kp = wload(wpgv[:, :, hs], 8, WPG)
                wl, kl = wload(wplev[:, :, hs], 2, ["wple_b"])
                for i in range(4):
                    b1 = rbank()
                    b2 = rbank()
                    for c in range(8):
                        OP("pe", "matmul", [hk[i], kp], ["pb%d" % b1], pb[b1][:, :], lhsT=hT4[:, c, i * 128:(i + 1) * 128], rhs=wp[:, c, :], start=(c == 0), stop=(c == 7))
                    for c in range(2):
                        OP("pe", "matmul", ["pT%d" % i, kl], ["pb%d" % b2], pb[b2][:, :], lhsT=pT[:, c, i * 128:(i + 1) * 128], rhs=wl[:, c, :], start=(c == 0), stop=(c == 1))
                    par = gi % 2
                    gi += 1
                    OP("act", "activation", ["pb%d" % b1], ["sga%d" % par], out=sga[par], in_=pb[b1][:, :], func=AF.Sigmoid)
                    OP("dve", "tensor_tensor", ["pb%d" % b2, "sga%d" % par], ["m1%d" % par], out=m1[par], in0=pb[b2][:, :], in1=sga[par], op=ALU.mult)
                    OP("pool", "tensor_tensor", ["m1%d" % par, "xtok%d" % i], ["xtok%d" % i], out=xtok[:, i, hs], in0=m1[par], in1=xtok[:, i, hs], op=ALU.add)
            for i in range(4):
                DMA("sp", out_d[tok0 + i * 128:tok0 + (i + 1) * 128, :], xtok[:, i, :], reads=["xtok%d" % i], writes=["out%d_%d" % (g, i)])

        if debug:
            P.barrier()
            DMA("pool", dbg["obT"][:, :], obT_d[:, :])
            DMA("pool", dbg["oaT"][:, :], oaT_d[:, :])
        P.final_wait("sp")
        P.emit(nc, ctx)
    return nc


N_CORES = 8
B, T, D = 4, 8192, 1024


def make_in_maps(inp, TC, TO, pairs):
    cst = host_consts()
    pv = host_pvec(inp)
    maps = []
    x = np.asarray(inp["x"], np.float32)
    p = np.asarray(inp["p"], np.float32)[0]
    pos = np.asarray(inp["positions"], np.int32)
    shared = {
        "cst": cst, "pv": pv, "cmask": host_cmask(),
        "w_in": np.ascontiguousarray(inp["w_in"][0], np.float32),
        "w_o_a": np.ascontiguousarray(inp["w_o_a"][0], np.float32),
        "w_o_b": np.ascontiguousarray(inp["w_o_b"][0], np.float32),
        "w_out": np.ascontiguousarray(inp["w_out"][0], np.float32),
        "w_up": np.ascontiguousarray(inp["w_up"][0], np.float32),
        "w_down": np.ascontiguousarray(inp["w_down"][0], np.float32),
        "w_pg": np.ascontiguousarray(inp["w_ple_gate"][0], np.float32),
        "w_ple": np.ascontiguousarray(inp["w_ple"][0], np.float32),
    }
    for (b, s) in pairs:
        xin = np.zeros((TC + TO, D), np.float32)
        posl = np.zeros((TC + TO,), np.int32)
        if s == 0:
            xin[TC:] = x[b, 0:TO]
            posl[TC:] = pos[b, 0:TO]
            ctxm = np.full((128, 1), -30000.0, np.float32)
            pin = p[b, 0:TO]
        else:
            xin[:] = x[b, 0:TC + TO]
            posl[:] = pos[b, 0:TC + TO]
            ctxm = np.zeros((128, 1), np.float32)
            pin = p[b, TC:TC + TO]
        m = dict(shared)
        m["xin"] = xin
        m["pin"] = np.ascontiguousarray(pin)
        m["pos"] = np.ascontiguousarray(posl.reshape(-1, 128).T)
        m["ctxm"] = ctxm
        maps.append(m)
    return maps


def kernel(**inp):
    TC = TO = T // 2
    nc = build(TC, TO)
    pairs = [(b, s) for b in range(B) for s in range(2)]
    maps = make_in_maps(inp, TC, TO, pairs)
    res = run_bass_kernel_spmd(nc, maps, core_ids=list(range(N_CORES)))
    out = np.zeros((B, T, D), np.float32)
    for i, (b, s) in enumerate(pairs):
        out[b, s * TO:(s + 1) * TO] = res.results[i]["out"]
    return out
```
